# Optimizing a Trainium2 kernel written in Bass

```python
import jax, jax.numpy as jnp
from jax import lax
import numpy as np

D_MODEL = 1024
BATCH = 32
SEQ = 2048
DEPTH = 1

N_HEADS = 16
HEAD_DIM = D_MODEL // N_HEADS
D_ATTN = N_HEADS * HEAD_DIM
BLOCK_Q = 128
SB_SCALE = HEAD_DIM ** -0.5
POOL_WINDOWS = (2, 4, 8, 16)
N_POOL_GROUPS = len(POOL_WINDOWS)
D_POOL = D_MODEL
POOL_GROUP_DIM = D_POOL // N_POOL_GROUPS
SPLITS = (D_ATTN, D_ATTN, D_ATTN, D_ATTN, D_POOL, D_POOL, D_MODEL, D_MODEL)
IN_COLS = sum(SPLITS)
EPS = 1e-6

kernel_name = "hybrid_stickbreak_pool_block"


def rms_norm(x, gain):
    xf = x.astype(jnp.float32)
    y = xf * lax.rsqrt(jnp.mean(xf * xf, axis=-1, keepdims=True) + EPS)
    return (y * gain.astype(jnp.float32)).astype(x.dtype)


def stick_breaking_attention(q, k, v):
    S = q.shape[2]
    outs = []
    for i in range(S // BLOCK_Q):
        q0, q1 = i * BLOCK_Q, (i + 1) * BLOCK_Q
        qb = q[:, :, q0:q1]
        kb = k[:, :, :q1]
        vb = v[:, :, :q1]
        z = jnp.einsum('bhqd,bhkd->bhqk', qb, kb).astype(jnp.float32) * SB_SCALE
        q_pos = jnp.arange(q0, q1)[:, None]
        k_pos = jnp.arange(q1)[None, :]
        mask = k_pos < q_pos
        log_1m_beta = jnp.where(mask, -jax.nn.softplus(z), 0.0)
        suffix = lax.cumsum(log_1m_beta, axis=3, reverse=True) - log_1m_beta
        w = jnp.where(mask, jnp.exp(jax.nn.log_sigmoid(z) + suffix), 0.0)
        outs.append(jnp.einsum('bhqk,bhkd->bhqd', w.astype(vb.dtype), vb))
    return jnp.concatenate(outs, axis=2)


def multiscale_causal_pool(u, pool_w, pool_scale):
    B, S, _ = u.shape
    uf = u.astype(jnp.float32)
    pos_count = jnp.arange(S) + 1
    diffs = []
    for g, win in enumerate(POOL_WINDOWS):
        ug = uf[:, :, g * POOL_GROUP_DIM:(g + 1) * POOL_GROUP_DIM]
        cs = jnp.cumsum(ug, axis=1)
        cs_shift = jnp.pad(cs, ((0, 0), (win, 0), (0, 0)))[:, :S]
        count = jnp.minimum(pos_count, win).astype(jnp.float32)[None, :, None]
        diffs.append((cs - cs_shift) / count - ug)
    d = jnp.stack(diffs, axis=2).astype(u.dtype)
    mixed = jnp.einsum('bsgc,gcd->bsgd', d, pool_w).reshape(B, S, D_POOL)
    return mixed * pool_scale


def hybrid_layer(x, c, norm_gain, w_ada, b_ada, w_in, pool_w, pool_scale,
                 w_branch_a, w_branch_b, w_out):
    B, S, D = x.shape
    mod = jax.nn.silu(c) @ w_ada + b_ada
    shift, scale, gate = jnp.split(mod, 3, axis=-1)
    h = rms_norm(x, norm_gain) * (1.0 + scale[:, None, :]) + shift[:, None, :]

    proj = h @ w_in
    offs = np.cumsum(SPLITS)[:-1].tolist()
    q, k, v, z_a, u, z_b, m_a, m_b = jnp.split(proj, offs, axis=-1)

    to_heads = lambda t: t.reshape(B, S, N_HEADS, HEAD_DIM).transpose(0, 2, 1, 3)
    attn = stick_breaking_attention(to_heads(q), to_heads(k), to_heads(v))
    attn = attn.transpose(0, 2, 1, 3).reshape(B, S, D_ATTN)
    p_a = (attn * jax.nn.silu(z_a)) @ w_branch_a

    pooled = multiscale_causal_pool(u, pool_w, pool_scale)
    p_b = (pooled * jax.nn.silu(z_b)) @ w_branch_b

    merged = jax.nn.sigmoid(m_a) * p_a + jax.nn.sigmoid(m_b) * p_b
    out = merged @ w_out
    return x + gate[:, None, :] * out


def setup_inputs(seed: int = 0) -> dict:
    key = jax.random.key(seed)
    ks = jax.random.split(key, 13)
    f32 = jnp.float32
    nrm = lambda k, shape, s: jax.random.normal(k, shape, f32) * s
    return {
        "x": nrm(ks[0], (BATCH, SEQ, D_MODEL), 1.0),
        "c": nrm(ks[1], (BATCH, D_MODEL), 1.0),
        "norm_gain": 1.0 + nrm(ks[2], (DEPTH, D_MODEL), 0.05),
        "w_ada": nrm(ks[3], (DEPTH, D_MODEL, 3 * D_MODEL), 0.5 * D_MODEL ** -0.5),
        "b_ada": nrm(ks[4], (DEPTH, 3 * D_MODEL), 0.01),
        "w_in": nrm(ks[5], (DEPTH, D_MODEL, IN_COLS), D_MODEL ** -0.5),
        "pool_w": nrm(ks[6], (DEPTH, N_POOL_GROUPS, POOL_GROUP_DIM, POOL_GROUP_DIM), POOL_GROUP_DIM ** -0.5),
        "pool_scale": 1.0 + nrm(ks[7], (DEPTH, D_POOL), 0.1),
        "w_branch_a": nrm(ks[8], (DEPTH, D_ATTN, D_MODEL), D_ATTN ** -0.5),
        "w_branch_b": nrm(ks[9], (DEPTH, D_POOL, D_MODEL), D_POOL ** -0.5),
        "w_out": nrm(ks[10], (DEPTH, D_MODEL, D_MODEL), D_MODEL ** -0.5),
        "final_gain": 1.0 + nrm(ks[11], (D_MODEL,), 0.05),
    }


def reference(x, c, norm_gain, w_ada, b_ada, w_in, pool_w, pool_scale,
              w_branch_a, w_branch_b, w_out, final_gain):
    for l in range(DEPTH):
        x = hybrid_layer(x, c, norm_gain[l], w_ada[l], b_ada[l], w_in[l], pool_w[l],
                         pool_scale[l], w_branch_a[l], w_branch_b[l], w_out[l])
    return rms_norm(x, final_gain)
```

```python
import numpy as np
from contextlib import ExitStack
import concourse.bass as bass
import concourse.mybir as mybir
from concourse.bass_utils import run_bass_kernel_spmd

F32 = mybir.dt.float32
BF16 = mybir.dt.bfloat16
AF = mybir.ActivationFunctionType
ALU = mybir.AluOpType

NCORES = 8
SEQ = 2048
D = 1024
NSEQ_CORE = 4
EPS = 1e-6
NCH = 4
NTT = 16
POOL_WINDOWS = (2, 4, 8, 16)
USE_SOFTPLUS = True

CB_ID, CB_MTRI, CB_TRI, CB_ES, CB_L = 0, 128, 256, 384, 384 + 16 * 192
CB_N = CB_L + 2048
CF_SGN, CF_MLO, CF_CNT = 0, 1, 2
CF_NH = 2 + 64
CF_EPS = 2 + 64 + 1
CF_N = 2 + 64 + 2


class Buf:
    __slots__ = ("w", "r", "dsem", "dcnt", "uid")
    _n = 0

    def __init__(self):
        self.w = None
        self.r = {}
        self.dsem = None
        self.dcnt = 0
        Buf._n += 1
        self.uid = Buf._n


class KB:
    def __init__(self, nc, es):
        self.nc = nc
        self.es = es
        self.eng = {"pe": nc.tensor, "act": nc.scalar, "dve": nc.vector, "pool": nc.gpsimd, "sp": nc.sync}
        self.sem = {k: es.enter_context(nc.semaphore("s_" + k)) for k in self.eng}
        self.cnt = {k: 0 for k in self.eng}
        self.waited = {}
        self.dbufs = []
        self.free_sems = {True: [], False: []}
        self.nsem = 0
        self.nwaits = 0

    def _wait(self, e, tok):
        if tok is None:
            return
        key, val = tok
        if isinstance(key, str):
            if key == e and e == "pe":
                return
            k = (e, key)
            sem = self.sem[key]
        else:
            if key.dsem is None or key.dcnt < val:
                return
            k = (e, key.uid)
            sem = key.dsem
        if self.waited.get(k, 0) >= val:
            return
        self.eng[e].wait_ge(sem, val)
        self.nwaits += 1
        self.waited[k] = val

    def _deps(self, e, reads, writes):
        for b in reads:
            self._wait(e, b.w)
        for b in writes:
            self._wait(e, b.w)
            for t in b.r.values():
                self._wait(e, t)

    def op(self, e, fn, reads=(), writes=(), sig=True):
        self._deps(e, reads, writes)
        inst = fn(self.eng[e])
        if sig:
            self.cnt[e] += 1
            inst.then_inc(self.sem[e], 1)
            tok = (e, self.cnt[e])
        else:
            tok = (e, self.cnt[e] + 1)
        for b in reads:
            b.r[e] = tok
        for b in writes:
            b.w = tok
            b.r = {}
        return inst

    def dma(self, q, out, in_, sb, reads=(), writes=()):
        self._deps(q, reads, writes)
        if sb.dsem is None:
            pool_ = self.free_sems[q == "pool"]
            if pool_:
                sb.dsem, sb.dcnt = pool_.pop()
            else:
                self.nsem += 1
                sb.dsem = self.es.enter_context(self.nc.semaphore("d%d" % self.nsem))
                sb.dcnt = 0
            self.dbufs.append((sb, q == "pool"))
        inst = self.eng[q].dma_start(out=out, in_=in_)
        sb.dcnt += 16
        inst.then_inc(sb.dsem, 16)
        tok = (sb, sb.dcnt)
        for b in reads:
            b.r[("dma", sb.uid)] = tok
        for b in writes:
            b.w = tok
            b.r = {}

    def barrier(self):
        for e in self.eng:
            for f in ("pe", "act", "dve", "pool"):
                if f != e and self.cnt[f] > 0:
                    self._wait(e, (f, self.cnt[f]))
            for b, _sw in self.dbufs:
                self._wait(e, (b, b.dcnt))
        for b, sw in self.dbufs:
            self.free_sems[sw].append((b.dsem, b.dcnt))
            b.dsem = None
            b.dcnt = -1
        self.dbufs = []


def build(nseq=NSEQ_CORE):
    nc = bass.Bass("TRN2", target_bir_lowering=False)
    NTOK = nseq * SEQ

    def din(name, shape):
        return nc.dram_tensor(name, list(shape), F32, kind="ExternalInput").ap()

    x_d = din("x", [NTOK, D])
    cT_d = din("cT", [128, 8, nseq])
    wada_d = din("w_ada_l", [128, 8, 3072])
    bada_d = din("b_ada", [1, 3072])
    ngain_d = din("norm_gain", [1, D])
    fgain_d = din("final_gain", [1, D])
    wp_d = din("Wp", [8, 128, 8, 4, 128])
    wpl_d = din("Wpl", [4, 128, 8, 4, 128])
    wm_d = din("Wm", [8, 128, 8, 2, 128])
    wab_d = din("Wab", [8, 128, 8, 2, 128])
    wout_d = din("w_out_l", [128, 8, 1024])
    poolw_d = din("pool_w_l", [128, 4, 2, 256])
    pscale_d = din("pool_scale_l", [128, 8])
    cb_d = din("constsB", [128, CB_N])
    cf_d = din("constsF", [128, CF_N])
    out_d = nc.dram_tensor("out", [NTOK, D], F32, kind="ExternalOutput").ap()
    mod_d = nc.dram_tensor("mod_scratch", [nseq, 3072], F32, kind="Internal").ap()

    with ExitStack() as es:
        kb = KB(nc, es)
        uid = [0]

        def sb(name, shape, dt, stack):
            uid[0] += 1
            return stack.enter_context(nc.sbuf_tensor("%s_%d" % (name, uid[0]), list(shape), dt))

        def psum_banks(stack, n, nbf=0):
            uid[0] += 1
            ps = [stack.enter_context(nc.psum_tensor("ps%d_%d" % (i, uid[0]), [128, 512], F32)) for i in range(n)]
            pb = [stack.enter_context(nc.psum_tensor("pb%d_%d" % (i, uid[0]), [128, 8, 128], BF16)) for i in range(nbf)]
            return ps, [Buf() for _ in range(n)], pb, [Buf() for _ in range(nbf)]

        cB = sb("cB", [128, CB_N], BF16, es)
        cF = sb("cF", [128, CF_N], F32, es)
        pscale = sb("pscale", [128, 8], F32, es)
        hT = sb("hT", [128, 8, SEQ], BF16, es)

        b_cB, b_cF = Buf(), Buf()
        b_const = Buf()
        b_hT = [Buf() for _ in range(NCH)]
        b_mod = Buf()

        ident = cB[:, CB_ID:CB_ID + 128]
        mtri = cB[:, CB_MTRI:CB_MTRI + 128]
        trineg = cB[:, CB_TRI:CB_TRI + 128]

        def es_half(a):
            o = CB_ES + a * 192 + 64
            return cB[:, o:o + 64]

        def mm2(e, out, lo, lhsT, rhs, start, stop):
            e.matmul(out[0:64, lo:512], lhsT[:, 0:64], rhs, start=start, stop=stop)
            return e.matmul(out[64:128, lo:512], lhsT[:, 64:128], rhs, start=start, stop=stop)

        kb.dma("pool", cB[:], cb_d[:, :], b_cB, writes=[b_cB])
        kb.dma("sp", cF[:], cf_d[:, :], b_cF, writes=[b_cF])
        kb.dma("sp", pscale[:], pscale_d[:, :], b_cF, writes=[b_cF])

        with ExitStack() as ph:
            PS, b_PS, _, _ = psum_banks(ph, 2)
            cTt = sb("cTt", [128, 8, nseq], F32, ph)
            scT = sb("scT", [128, 8, nseq], BF16, ph)
            wad = [sb("wad%d" % i, [128, 8, 1024], BF16, ph) for i in range(3)]
            bad = sb("bad", [nseq, 3072], F32, ph)
            modS = sb("modS", [nseq, 3072], F32, ph)
            b_c, b_sc, b_bad, b_modS = Buf(), Buf(), Buf(), Buf()
            b_wad = [Buf() for _ in range(3)]
            kb.dma("sp", cTt[:], cT_d[:, :, :], b_c, writes=[b_c])
            kb.dma("sp", bad[:], bada_d[0:1, :].partition_broadcast(nseq), b_bad, writes=[b_bad])
            for kind in range(3):
                kb.dma("pool", wad[kind][:], wada_d[:, :, kind * 1024:(kind + 1) * 1024], b_wad[kind], writes=[b_wad[kind]])
            kb.op("act", lambda e: e.activation(out=scT[:], in_=cTt[:], func=AF.Silu), reads=[b_c], writes=[b_sc])
            for kind in range(3):
                for half in range(2):
                    pb = (kind * 2 + half) % 2
                    pst = PS[pb]
                    for kt in range(8):
                        kb.op("pe", lambda e: e.matmul(pst[0:nseq, :], scT[:, kt, :], wad[kind][:, kt, half * 512:(half + 1) * 512],
                                                       start=(kt == 0), stop=(kt == 7)),
                              reads=[b_sc, b_wad[kind]], writes=[b_PS[pb]], sig=(kt == 7))
                    col = kind * 1024 + half * 512
                    kb.op("dve", lambda e: e.tensor_tensor(out=modS[:, col:col + 512], in0=pst[0:nseq, :], in1=bad[:, col:col + 512], op=ALU.add),
                          reads=[b_PS[pb], b_bad], writes=[b_modS])
            kb.dma("sp", mod_d[:, :], modS[:], b_modS, reads=[b_modS], writes=[b_mod])
            kb.barrier()

        for bl in range(nseq):
            tok0 = bl * SEQ
            with ExitStack() as ph:
                _, _, PB, b_PB = psum_banks(ph, 0, 2)
                G = sb("G", [128, D], F32, ph)
                gain_bc = sb("gain_bc", [128, D], F32, ph)
                shift = sb("shift", [128, D], F32, ph)
                sctmp = sb("sctmp", [128, D], F32, ph)
                xt = [sb("xt%d" % i, [128, D], F32, ph) for i in range(4)]
                junk = sb("junk", [128, D], BF16, ph)
                t1 = [sb("t1%d" % i, [128, D], F32, ph) for i in range(2)]
                hb = [sb("hb%d" % i, [128, D], BF16, ph) for i in range(2)]
                ss = sb("ss", [128, NTT], F32, ph)
                rstd = sb("rstd", [128, NTT], F32, ph)
                b_G, b_shift, b_sct, b_gain, b_junk = Buf(), Buf(), Buf(), Buf(), Buf()
                b_xt = [Buf() for _ in range(4)]
                b_t1 = [Buf(), Buf()]
                b_hb = [Buf(), Buf()]
                b_ss = [Buf() for _ in range(NTT)]
                b_rstd = [Buf() for _ in range(NTT)]
                kb.dma("sp", gain_bc[:], ngain_d[0:1, :].partition_broadcast(128), b_gain, writes=[b_gain])
                kb.dma("sp", shift[:], mod_d[bl:bl + 1, 0:1024].partition_broadcast(128), b_shift, reads=[b_mod], writes=[b_shift])
                kb.dma("sp", sctmp[:], mod_d[bl:bl + 1, 1024:2048].partition_broadcast(128), b_sct, reads=[b_mod], writes=[b_sct])
                kb.op("dve", lambda e: e.scalar_tensor_tensor(out=G[:], in0=sctmp[:], scalar=1.0, in1=gain_bc[:], op0=ALU.add, op1=ALU.mult),
                      reads=[b_sct, b_gain], writes=[b_G])
                def x_stage1(tt):
                    s = tt % 4
                    kb.dma("sp", xt[s][:], x_d[tok0 + tt * 128: tok0 + (tt + 1) * 128, :], b_xt[s], writes=[b_xt[s]])
                    kb.op("act", lambda e: e.activation(out=junk[:], in_=xt[s][:], func=AF.Square, accum_out=ss[:, tt:tt + 1]),
                          reads=[b_xt[s]], writes=[b_junk, b_ss[tt]])
                    kb.op("act", lambda e: e.activation(out=rstd[:, tt:tt + 1], in_=ss[:, tt:tt + 1], func=AF.Ln, bias=cF[:, CF_EPS:CF_EPS + 1], scale=1.0 / D),
                          reads=[b_ss[tt]], writes=[b_rstd[tt]])
                    kb.op("act", lambda e: e.activation(out=rstd[:, tt:tt + 1], in_=rstd[:, tt:tt + 1], func=AF.Exp, scale=-0.5),
                          reads=[b_rstd[tt]], writes=[b_rstd[tt]])

                def x_stage2(tt):
                    s = tt % 2
                    s4 = tt % 4
                    kb.op("dve", lambda e: e.scalar_tensor_tensor(out=t1[s][:], in0=xt[s4][:], scalar=rstd[:, tt:tt + 1], in1=G[:],
                                                                  op0=ALU.mult, op1=ALU.mult),
                          reads=[b_xt[s4], b_rstd[tt], b_G], writes=[b_t1[s]])
                    kb.op("pool", lambda e: e.tensor_tensor(out=hb[s][:], in0=t1[s][:], in1=shift[:], op=ALU.add),
                          reads=[b_t1[s], b_shift], writes=[b_hb[s]])

                def x_stage3(tt):
                    s = tt % 2
                    for kt in range(8):
                        kb.op("pe", lambda e: e.transpose(PB[s][:, kt, :], hb[s][:, kt * 128:(kt + 1) * 128], ident),
                              reads=[b_hb[s]], writes=[b_PB[s]], sig=(kt == 7))
                    kb.op("act", lambda e: e.activation(out=hT[:, :, tt * 128:(tt + 1) * 128], in_=PB[s][:, :, :], func=AF.Copy),
                          reads=[b_PB[s]], writes=[b_hT[tt // 4]])

                for i in range(NTT + 2):
                    if i < NTT:
                        x_stage1(i)
                    if 0 <= i - 1 < NTT:
                        x_stage2(i - 1)
                    if 0 <= i - 2 < NTT:
                        x_stage3(i - 2)
                kb.barrier()

            with ExitStack() as phga:
                ga = sb("ga", [128, 8, SEQ], BF16, phga)
                b_ga = [[Buf() for _ in range(NCH)] for _ in range(8)]
                with ExitStack() as ph:
                    PS, b_PS, _, _ = psum_banks(ph, 8)
                    PB = PS[7][:].bitcast(BF16).rearrange("p (k c) -> p k c", k=8)
                    b_PB = b_PS[7]
                    NSL = 2
                    wp = [sb("wp%d" % i, [128, 8, 4, 128], BF16, ph) for i in range(NSL)]
                    qa = [[sb("qa%d_%d" % (i, h), [128, SEQ], BF16, ph) for h in range(2)] for i in range(NSL)]
                    ka = [[sb("ka%d_%d" % (i, h), [128, SEQ], BF16, ph) for h in range(2)] for i in range(NSL)]
                    zg = [sb("zg%d" % i, [128, SEQ], F32, ph) for i in range(NSL)]
                    vT = sb("vT", [128, SEQ], BF16, ph)
                    vt = [sb("vt%d" % i, [128, NTT, 128], BF16, ph) for i in range(NSL)]
                    sp = [sb("sp%d" % h, [128, 16, 512], BF16, ph) for h in range(2)]
                    NW = 4
                    wt = [sb("wt%d" % i, [128, 512], BF16, ph) for i in range(NW)]
                    HS = sb("HS", [128, 512], BF16, ph)
                    etmp = [] if USE_SOFTPLUS else [sb("etmp%d" % i, [128, 512], F32, ph) for i in range(3)]
                    b_etmp = [Buf() for _ in range(3)]
                    b_wp = [Buf() for _ in range(NSL)]
                    b_qa = [[[Buf() for _ in range(NCH)] for _ in range(2)] for _ in range(NSL)]
                    b_ka = [[[Buf() for _ in range(NCH)] for _ in range(2)] for _ in range(NSL)]
                    b_zg = [[Buf() for _ in range(NCH)] for _ in range(NSL)]
                    b_vT = [Buf() for _ in range(NCH)]
                    b_vt = [[Buf(), Buf()] for _ in range(NSL)]
                    b_sp = [[Buf() for _ in range(16)] for _ in range(2)]
                    b_wt = [Buf() for _ in range(NW)]
                    b_HS = Buf()
                    kb.dma("pool", wp[0][:], wp_d[0], b_wp[0], writes=[b_wp[0]])
                    for i in range(NSL):
                        kb.dma("pool", ka[i][0][64:128, :], cb_d[64:128, CB_L:CB_L + 2048], Buf(), writes=[b_ka[i][0][c] for c in range(NCH)])
                        kb.dma("pool", ka[i][1][0:64, :], cb_d[0:64, CB_L:CB_L + 2048], Buf(), writes=[b_ka[i][1][c] for c in range(NCH)])
                    state = {"ww": 0, "pp": 0}
                    XB = [1, 2, 3]
                    PROJB = [0, 7]

                    def proj_pair(j, overlapped):
                        sl = j % NSL
                        if j > 0:
                            kb.dma("pool", wp[sl][:], wp_d[j], b_wp[sl], writes=[b_wp[sl]])
                        kb.op("pool", lambda e: e.memset(qa[sl][0][64:128, :], 0.0), writes=[b_qa[sl][0][c] for c in range(NCH)])
                        kb.op("pool", lambda e: e.memset(qa[sl][1][0:64, :], 0.0), writes=[b_qa[sl][1][c] for c in range(NCH)])
                        for g in range(4):
                            for c in range(NCH):
                                pi = PROJB[state["pp"] % 2]
                                state["pp"] += 1
                                pst = PS[pi]
                                for kt in range(8):
                                    kb.op("pe", lambda e: e.matmul(pst[:], wp[sl][:, kt, g, :], hT[:, kt, c * 512:(c + 1) * 512],
                                                                   start=(kt == 0), stop=(kt == 7)),
                                          reads=[b_wp[sl], b_hT[c]], writes=[b_PS[pi]], sig=(kt == 7))
                                    if overlapped and kt == 3:
                                        yield
                                cs = slice(c * 512, (c + 1) * 512)
                                if g == 0 or g == 1:
                                    T, bT = (qa, b_qa) if g == 0 else (ka, b_ka)
                                    kb.op("dve", lambda e: e.tensor_copy(out=T[sl][0][0:64, cs], in_=pst[0:64, :]),
                                          reads=[b_PS[pi]], writes=[bT[sl][0][c]])
                                    kb.op("dve", lambda e: e.tensor_copy(out=T[sl][1][64:128, cs], in_=pst[64:128, :]),
                                          reads=[b_PS[pi]], writes=[bT[sl][1][c]])
                                elif g == 2:
                                    kb.op("dve", lambda e: e.tensor_copy(out=vT[:, cs], in_=pst[:]),
                                          reads=[b_PS[pi]], writes=[b_vT[c]])
                                else:
                                    kb.op("dve", lambda e: e.tensor_copy(out=zg[sl][:, cs], in_=pst[:]),
                                          reads=[b_PS[pi]], writes=[b_zg[sl][c]])
                                yield
                        for hf in range(2):
                            for k8 in range(8):
                                kbi = hf * 8 + k8
                                kb.op("pe", lambda e: e.transpose(PB[:, k8, :], vT[:, kbi * 128:(kbi + 1) * 128], ident),
                                      reads=[b_vT[kbi // 4]], writes=[b_PB], sig=(k8 == 7))
                            kb.op("dve", lambda e: e.tensor_copy(out=vt[sl][:, hf * 8:(hf + 1) * 8, :], in_=PB[:, :, :]),
                                  reads=[b_PB], writes=[b_vt[sl][hf]])
                            yield

                    def silu_zg(sl):
                        kb.op("act", lambda e: e.activation(out=zg[sl][:, :], in_=zg[sl][:, :], func=AF.Silu),
                              reads=[b_zg[sl][c] for c in range(NCH)], writes=[b_zg[sl][c] for c in range(NCH)])

                    def attn_pair(j):
                        sl = j % NSL
                        SK = 2
                        for c in range(NCH):
                            n = 4 * c + 4
                            cbase = c * 512
                            blocks = [(hd, a) for a in range(n) for hd in range(2)]
                            NB = len(blocks)

                            def lo_of(a):
                                return max(0, a - 4 * c) * 128

                            def p1_front(i):
                                hd, a = blocks[i]
                                lo = lo_of(a)
                                xi = XB[i % len(XB)]
                                X = PS[xi]
                                kb.op("pe", lambda e: mm2(e, X, lo, ka[sl][hd][:, a * 128:(a + 1) * 128],
                                                          qa[sl][hd][:, cbase + lo:cbase + 512], True, True),
                                      reads=[b_ka[sl][hd][a // 4], b_qa[sl][hd][c]], writes=[b_PS[xi]])
                                if USE_SOFTPLUS:
                                    kb.op("act", lambda e: e.activation(out=sp[hd][:, a, lo:512], in_=X[:, lo:512], func=AF.Softplus, scale=0.125),
                                          reads=[b_PS[xi]], writes=[b_sp[hd][a]])
                                else:
                                    ei = i % 3
                                    kb.op("act", lambda e: e.activation(out=etmp[ei][:, lo:512], in_=X[:, lo:512], func=AF.Exp, scale=0.125),
                                          reads=[b_PS[xi]], writes=[b_etmp[ei]])
                                    kb.op("act", lambda e: e.activation(out=sp[hd][:, a, lo:512], in_=etmp[ei][:, lo:512], func=AF.Ln, bias=1.0, scale=1.0),
                                          reads=[b_etmp[ei]], writes=[b_sp[hd][a]])
                                if a >= 4 * c:
                                    kb.op("dve", lambda e: e.tensor_tensor(out=sp[hd][:, a, lo:lo + 128], in0=sp[hd][:, a, lo:lo + 128],
                                                                           in1=mtri, op=ALU.mult),
                                          reads=[b_sp[hd][a]], writes=[b_sp[hd][a]])

                            def p1_back(i):
                                hd, a = blocks[i]
                                lo = lo_of(a)
                                trows = slice(64, 128) if hd == 0 else slice(0, 64)
                                kb.op("pe", lambda e: e.matmul(PS[4][trows, lo:512], es_half(a), sp[hd][:, a, lo:512],
                                                               start=(a == 0), stop=(a == n - 1)),
                                      reads=[b_sp[hd][a]], writes=[b_PS[4]], sig=(i == NB - 1))

                            for i in range(NB + SK):
                                if i < NB:
                                    p1_front(i)
                                if i - SK >= 0:
                                    p1_back(i - SK)
                                yield "step1"
                            kb.op("dve", lambda e: e.tensor_scalar(out=HS[:], in0=PS[4][:], scalar1=cF[:, CF_SGN:CF_SGN + 1], scalar2=None,
                                                                   op0=ALU.mult), reads=[b_PS[4]], writes=[b_HS])
                            kb.op("dve", lambda e: e.scalar_tensor_tensor(out=qa[sl][0][64:128, cbase:cbase + 512], in0=PS[4][64:128, :],
                                                                          scalar=cF[64:128, CF_MLO:CF_MLO + 1], in1=HS[64:128, :],
                                                                          op0=ALU.mult, op1=ALU.add),
                                  reads=[b_PS[4], b_HS], writes=[b_qa[sl][0][c]])
                            kb.op("dve", lambda e: e.scalar_tensor_tensor(out=qa[sl][1][0:64, cbase:cbase + 512], in0=PS[4][0:64, :],
                                                                          scalar=cF[0:64, CF_MLO:CF_MLO + 1], in1=HS[0:64, :],
                                                                          op0=ALU.mult, op1=ALU.add),
                                  reads=[b_PS[4], b_HS], writes=[b_qa[sl][1][c]])
                            if c == 0:
                                silu_zg(sl)
                            yield "drain"
                            wslot = {}

                            def p3_front(i):
                                hd, a = blocks[i]
                                lo = lo_of(a)
                                xi = XB[i % len(XB)]
                                X = PS[xi]
                                kb.op("pe", lambda e: mm2(e, X, lo, ka[sl][hd][:, a * 128:(a + 1) * 128],
                                                          qa[sl][hd][:, cbase + lo:cbase + 512], True, False),
                                      reads=[b_ka[sl][hd][a // 4], b_qa[sl][hd][c]], writes=[b_PS[xi]], sig=False)
                                kb.op("pe", lambda e: mm2(e, X, lo, trineg, sp[hd][:, a, lo:512], False, True),
                                      reads=[b_sp[hd][a]], writes=[b_PS[xi]])
                                wi = state["ww"] % NW
                                state["ww"] += 1
                                wslot[i] = wi
                                kb.op("act", lambda e: e.activation(out=wt[wi][:, lo:512], in_=X[:, lo:512], func=AF.Exp, scale=0.125),
                                      reads=[b_PS[xi]], writes=[b_wt[wi]])
                                if a >= 4 * c:
                                    kb.op("dve", lambda e: e.tensor_tensor(out=wt[wi][:, lo:lo + 128], in0=wt[wi][:, lo:lo + 128],
                                                                           in1=mtri, op=ALU.mult),
                                          reads=[b_wt[wi]], writes=[b_wt[wi]])

                            def p3_back(i):
                                hd, a = blocks[i]
                                lo = lo_of(a)
                                wi = wslot[i]
                                oi = 5 + hd
                                rows = slice(0, 64) if hd == 0 else slice(64, 128)
                                kb.op("pe", lambda e: e.matmul(PS[oi][rows, lo:512], vt[sl][:, a, rows], wt[wi][:, lo:512],
                                                               start=(a == 0), stop=(a == n - 1)),
                                      reads=[b_vt[sl][a // 8], b_wt[wi]], writes=[b_PS[oi]], sig=(a == n - 1))
                                if a == n - 1:
                                    kb.op("dve", lambda e: e.tensor_tensor(out=ga[rows, j, cbase:cbase + 512], in0=PS[oi][rows, :],
                                                                           in1=zg[sl][rows, cbase:cbase + 512], op=ALU.mult),
                                          reads=[b_PS[oi], b_zg[sl][c]], writes=[b_ga[j][c]])

                            for i in range(NB + SK):
                                if i < NB:
                                    p3_front(i)
                                if i - SK >= 0:
                                    p3_back(i - SK)
                                yield "step"
                            yield "drain"

                    def run_interleaved(gmain, gside, ratio, ndrain):
                        k = 0
                        side_done = gside is None

                        def side(nsteps):
                            nonlocal side_done
                            for _ in range(nsteps):
                                if side_done:
                                    return
                                try:
                                    next(gside)
                                except StopIteration:
                                    side_done = True

                        for tag in gmain:
                            if tag == "drain":
                                side(ndrain)
                            elif tag == "step1":
                                k += 1
                                if k % ratio == 0:
                                    side(1)
                        if not side_done:
                            for _ in gside:
                                pass

                    for _ in proj_pair(0, False):
                        pass
                    for j in range(8):
                        nxt = proj_pair(j + 1, True) if j + 1 < 8 else None
                        run_interleaved(attn_pair(j), nxt, 1000, 5)
                    kb.barrier()

                with ExitStack() as phgb:
                    gb = sb("gb", [128, 8, SEQ], BF16, phgb)
                    b_gb = [[Buf() for _ in range(NCH)] for _ in range(8)]
                    with ExitStack() as ph:
                        PS, b_PS, _, _ = psum_banks(ph, 6)
                        wg = [sb("wg%d" % i, [128, 8, 4, 128], BF16, ph) for i in range(2)]
                        poolw = sb("poolw", [128, 4, 2, 256], BF16, ph)
                        U = [[sb("U%d_%d" % (s_, ci), [128, 16 + 512], F32, ph) for ci in range(2)] for s_ in range(2)]
                        SA = [sb("SA%d" % ci, [128, 16 + 512], F32, ph) for ci in range(2)]
                        SB_ = [sb("SB%d" % ci, [128, 16 + 512], F32, ph) for ci in range(2)]
                        dT = [[sb("dT%d_%d" % (s_, ci), [128, 512], BF16, ph) for ci in range(2)] for s_ in range(2)]
                        zbs = [[sb("zbs%d_%d" % (s_, dt_), [128, 512], F32, ph) for dt_ in range(2)] for s_ in range(2)]
                        fx = sb("fx", [128, 16], F32, ph)
                        b_wg = [Buf(), Buf()]
                        b_poolw, b_fx = Buf(), Buf()
                        b_U = [[Buf(), Buf()], [Buf(), Buf()]]
                        b_SA = [Buf(), Buf()]
                        b_SB = [Buf(), Buf()]
                        b_dT = [[Buf(), Buf()], [Buf(), Buf()]]
                        b_zbs = [[Buf(), Buf()], [Buf(), Buf()]]
                        kb.dma("pool", poolw[:], poolw_d[:, :, :, :], b_poolw, writes=[b_poolw])
                        state_p = {"pp": 0}

                        def p_stage_a(g, c, s_):
                            win = POOL_WINDOWS[g]
                            gs = g % 2
                            cs = slice(c * 512, (c + 1) * 512)
                            if c == 0:
                                kb.dma("pool", wg[gs][:], wpl_d[g], b_wg[gs], writes=[b_wg[gs]])
                                for ci in range(2):
                                    kb.op("pool", lambda e: e.memset(U[s_][ci][:, 0:16], 0.0), writes=[b_U[s_][ci]])
                            for which in range(4):
                                pi = state_p["pp"] % 4
                                state_p["pp"] += 1
                                pst = PS[pi]
                                for kt in range(8):
                                    kb.op("pe", lambda e: e.matmul(pst[:], wg[gs][:, kt, which, :], hT[:, kt, cs],
                                                                   start=(kt == 0), stop=(kt == 7)),
                                          reads=[b_wg[gs], b_hT[c]], writes=[b_PS[pi]], sig=(kt == 7))
                                if which < 2:
                                    ci = which
                                    kb.op("act", lambda e: e.activation(out=U[s_][ci][:, 16:528], in_=pst[:], func=AF.Copy),
                                          reads=[b_PS[pi]], writes=[b_U[s_][ci]])
                                else:
                                    dt_ = which - 2
                                    kb.op("act", lambda e: e.activation(out=zbs[s_][dt_][:], in_=pst[:], func=AF.Silu),
                                          reads=[b_PS[pi]], writes=[b_zbs[s_][dt_]])
                            for ci in range(2):
                                eng = "pool" if ci == 0 else "dve"
                                u = U[s_][ci]
                                bu = b_U[s_][ci]
                                if c + 1 < NCH:
                                    kb.op("pool", lambda e: e.tensor_copy(out=U[1 - s_][ci][:, 0:16], in_=u[:, 512:528]),
                                          reads=[bu], writes=[b_U[1 - s_][ci]])
                                cur, bcur = u, bu
                                nxts = [(SA[ci], b_SA[ci]), (SB_[ci], b_SB[ci])]
                                first = 0
                                for step in range(g + 1):
                                    sh = 1 << step
                                    nx, bnx = nxts[step % 2]
                                    f0 = first + sh
                                    kb.op(eng, lambda e: e.tensor_tensor(out=nx[:, f0:528], in0=cur[:, f0:528],
                                                                         in1=cur[:, f0 - sh:528 - sh], op=ALU.add),
                                          reads=[bcur], writes=[bnx])
                                    cur, bcur = nx, bnx
                                    first = f0
                                kb.op("dve", lambda e: e.scalar_tensor_tensor(out=dT[s_][ci][:], in0=cur[:, 16:528], scalar=1.0 / win,
                                                                              in1=u[:, 16:528], op0=ALU.mult, op1=ALU.subtract),
                                      reads=[bcur, bu], writes=[b_dT[s_][ci]])
                                if c == 0:
                                    kb.op("dve", lambda e: e.tensor_tensor(out=fx[:], in0=cur[:, 16:32],
                                                                           in1=cF[:, CF_CNT + g * 16:CF_CNT + (g + 1) * 16], op=ALU.mult),
                                          reads=[bcur], writes=[b_fx])
                                    kb.op("dve", lambda e: e.tensor_tensor(out=dT[s_][ci][:, 0:16], in0=fx[:], in1=u[:, 16:32], op=ALU.subtract),
                                          reads=[b_fx, bu, b_dT[s_][ci]], writes=[b_dT[s_][ci]])

                        def p_stage_b(g, c, s_):
                            cs = slice(c * 512, (c + 1) * 512)
                            for dt_ in range(2):
                                ft = 2 * g + dt_
                                pi = 4 + dt_
                                pst = PS[pi]
                                for ci in range(2):
                                    kb.op("pe", lambda e: e.matmul(pst[:], poolw[:, g, ci, dt_ * 128:(dt_ + 1) * 128], dT[s_][ci][:],
                                                                   start=(ci == 0), stop=(ci == 1)),
                                          reads=[b_poolw, b_dT[s_][ci]], writes=[b_PS[pi]], sig=(ci == 1))
                                kb.op("dve", lambda e: e.scalar_tensor_tensor(out=gb[:, ft, cs], in0=pst[:],
                                                                              scalar=pscale[:, ft:ft + 1], in1=zbs[s_][dt_][:],
                                                                              op0=ALU.mult, op1=ALU.mult),
                                      reads=[b_PS[pi], b_zbs[s_][dt_]], writes=[b_gb[ft][c]])

                        steps = [(g, c) for g in range(4) for c in range(NCH)]
                        for k in range(len(steps) + 1):
                            if k < len(steps):
                                p_stage_a(steps[k][0], steps[k][1], k % 2)
                            if k - 1 >= 0:
                                p_stage_b(steps[k - 1][0], steps[k - 1][1], (k - 1) % 2)
                        kb.barrier()

                    with ExitStack() as phf:
                        mg = sb("mg", [128, 8, SEQ], BF16, phf)
                        wout = sb("wout", [128, 8, 1024], BF16, phf)
                        b_mg = [[Buf() for _ in range(NCH)] for _ in range(8)]
                        b_wout = Buf()
                        kb.dma("pool", wout[:], wout_d[:, :, :], b_wout, writes=[b_wout])
                        with ExitStack() as ph:
                            PS, b_PS, _, _ = psum_banks(ph, 8)
                            NS3 = 3
                            wab = [sb("wab%d" % i, [128, 8, 2, 128], BF16, ph) for i in range(NS3)]
                            wm = [sb("wm%d" % i, [128, 8, 2, 128], BF16, ph) for i in range(NS3)]
                            sgA = [sb("sgA%d" % i, [128, 512], F32, ph) for i in range(2)]
                            sgB = [sb("sgB%d" % i, [128, 512], F32, ph) for i in range(2)]
                            tA = [sb("tA%d" % i, [128, 512], F32, ph) for i in range(2)]
                            tB = [sb("tB%d" % i, [128, 512], F32, ph) for i in range(2)]
                            b_wab = [Buf() for _ in range(NS3)]
                            b_wm = [Buf() for _ in range(NS3)]
                            b_sgA, b_sgB, b_tA, b_tB = [Buf(), Buf()], [Buf(), Buf()], [Buf(), Buf()], [Buf(), Buf()]

                            def load_m(m):
                                s3 = m % NS3
                                kb.dma("pool", wab[s3][:], wab_d[m], b_wab[s3], writes=[b_wab[s3]])
                                kb.dma("pool", wm[s3][:], wm_d[m], b_wm[s3], writes=[b_wm[s3]])

                            load_m(0)
                            load_m(1)
                            step = 0
                            for m in range(8):
                                s3 = m % NS3
                                if m + 2 < 8:
                                    load_m(m + 2)
                                for c in range(NCH):
                                    cs = slice(c * 512, (c + 1) * 512)
                                    st2 = step % 2
                                    step += 1
                                    pA, pB_, pMa, pMb = [4 * st2 + k_ for k_ in range(4)]
                                    for kt in range(8):
                                        kb.op("pe", lambda e: e.matmul(PS[pMa][:], wm[s3][:, kt, 0, :], hT[:, kt, cs], start=(kt == 0), stop=(kt == 7)),
                                              reads=[b_wm[s3], b_hT[c]], writes=[b_PS[pMa]], sig=(kt == 7))
                                    for kt in range(8):
                                        kb.op("pe", lambda e: e.matmul(PS[pMb][:], wm[s3][:, kt, 1, :], hT[:, kt, cs], start=(kt == 0), stop=(kt == 7)),
                                              reads=[b_wm[s3], b_hT[c]], writes=[b_PS[pMb]], sig=(kt == 7))
                                    for kt in range(8):
                                        kb.op("pe", lambda e: e.matmul(PS[pA][:], wab[s3][:, kt, 0, :], ga[:, kt, cs], start=(kt == 0), stop=(kt == 7)),
                                              reads=[b_wab[s3], b_ga[kt][c]], writes=[b_PS[pA]], sig=(kt == 7))
                                    for kt in range(8):
                                        kb.op("pe", lambda e: e.matmul(PS[pB_][:], wab[s3][:, kt, 1, :], gb[:, kt, cs], start=(kt == 0), stop=(kt == 7)),
                                              reads=[b_wab[s3], b_gb[kt][c]], writes=[b_PS[pB_]], sig=(kt == 7))
                                    kb.op("act", lambda e: e.activation(out=sgA[st2][:], in_=PS[pMa][:], func=AF.Sigmoid),
                                          reads=[b_PS[pMa]], writes=[b_sgA[st2]])
                                    kb.op("act", lambda e: e.activation(out=sgB[st2][:], in_=PS[pMb][:], func=AF.Sigmoid),
                                          reads=[b_PS[pMb]], writes=[b_sgB[st2]])
                                    kb.op("dve", lambda e: e.tensor_tensor(out=tA[st2][:], in0=PS[pA][:], in1=sgA[st2][:], op=ALU.mult),
                                          reads=[b_PS[pA], b_sgA[st2]], writes=[b_tA[st2]])
                                    kb.op("dve", lambda e: e.tensor_tensor(out=tB[st2][:], in0=PS[pB_][:], in1=sgB[st2][:], op=ALU.mult),
                                          reads=[b_PS[pB_], b_sgB[st2]], writes=[b_tB[st2]])
                                    kb.op("pool", lambda e: e.tensor_tensor(out=mg[:, m, cs], in0=tA[st2][:], in1=tB[st2][:], op=ALU.add),
                                          reads=[b_tA[st2], b_tB[st2]], writes=[b_mg[m][c]])
                            kb.barrier()
                        with ExitStack() as ph:
                            PS, b_PS, _, _ = psum_banks(ph, 4)
                            fgain_bc = sb("fgain_bc", [128, D], F32, ph)
                            gate_bc = sb("gate_bc", [128, D], F32, ph)
                            xr = [sb("xr%d" % i, [128, D], F32, ph) for i in range(4)]
                            yy = [sb("yy%d" % i, [128, D], F32, ph) for i in range(5)]
                            junk2 = sb("junk2", [128, D], BF16, ph)
                            ss2 = sb("ss2", [128, NTT], F32, ph)
                            rs2 = sb("rs2", [128, NTT], F32, ph)
                            b_gain, b_gate, b_junk2 = Buf(), Buf(), Buf()
                            b_xr = [Buf() for _ in range(4)]
                            b_yy = [Buf() for _ in range(5)]
                            b_ss2 = [Buf() for _ in range(NTT)]
                            b_rs2 = [Buf() for _ in range(NTT)]
                            kb.dma("sp", fgain_bc[:], fgain_d[0:1, :].partition_broadcast(128), b_gain, writes=[b_gain])
                            kb.dma("sp", gate_bc[:], mod_d[bl:bl + 1, 2048:3072].partition_broadcast(128), b_gate, reads=[b_mod], writes=[b_gate])
                            NY = 5

                            def f2_load(tt):
                                s = tt % 4
                                r0 = tok0 + tt * 128
                                kb.dma("sp", xr[s][:], x_d[r0:r0 + 128, :], b_xr[s], writes=[b_xr[s]])

                            def f2_a(tt):
                                c = tt // 4
                                s = tt % 4
                                y3 = tt % NY
                                if tt == 0:
                                    for t0_ in range(3):
                                        f2_load(t0_)
                                if tt + 3 < NTT:
                                    f2_load(tt + 3)
                                for half in range(2):
                                    pi = 2 * (tt % 2) + half
                                    for m in range(8):
                                        kb.op("pe", lambda e: e.matmul(PS[pi][:], mg[:, m, tt * 128:(tt + 1) * 128], wout[:, m, half * 512:(half + 1) * 512],
                                                                       start=(m == 0), stop=(m == 7)),
                                              reads=[b_mg[m][c], b_wout], writes=[b_PS[pi]], sig=(m == 7))
                                    hs = slice(half * 512, (half + 1) * 512)
                                    kb.op("dve", lambda e: e.tensor_tensor(out=yy[y3][:, hs], in0=PS[pi][:], in1=gate_bc[:, hs], op=ALU.mult),
                                          reads=[b_PS[pi], b_gate], writes=[b_yy[y3]])
                                kb.op("pool", lambda e: e.tensor_tensor(out=yy[y3][:], in0=yy[y3][:], in1=xr[s][:], op=ALU.add),
                                      reads=[b_yy[y3], b_xr[s]], writes=[b_yy[y3]])
                                kb.op("act", lambda e: e.activation(out=junk2[:], in_=yy[y3][:], func=AF.Square, accum_out=ss2[:, tt:tt + 1]),
                                      reads=[b_yy[y3]], writes=[b_junk2, b_ss2[tt]])

                            def f2_b(tt):
                                kb.op("act", lambda e: e.activation(out=rs2[:, tt:tt + 1], in_=ss2[:, tt:tt + 1], func=AF.Ln, bias=cF[:, CF_EPS:CF_EPS + 1], scale=1.0 / D),
                                      reads=[b_ss2[tt]], writes=[b_rs2[tt]])
                                kb.op("act", lambda e: e.activation(out=rs2[:, tt:tt + 1], in_=rs2[:, tt:tt + 1], func=AF.Exp, scale=-0.5),
                                      reads=[b_rs2[tt]], writes=[b_rs2[tt]])
                                y3 = tt % NY
                                kb.op("act", lambda e: e.activation(out=yy[y3][:], in_=yy[y3][:], func=AF.Identity, scale=rs2[:, tt:tt + 1]),
                                      reads=[b_yy[y3], b_rs2[tt]], writes=[b_yy[y3]])

                            def f2_c(tt):
                                y3 = tt % NY
                                r0 = tok0 + tt * 128
                                kb.op("dve", lambda e: e.tensor_tensor(out=yy[y3][:], in0=yy[y3][:], in1=fgain_bc[:], op=ALU.mult),
                                      reads=[b_yy[y3], b_gain], writes=[b_yy[y3]])
                                kb.dma("sp", out_d[r0:r0 + 128, :], yy[y3][:], b_yy[y3], reads=[b_yy[y3]])

                            for i in range(NTT + 2):
                                if i < NTT:
                                    f2_a(i)
                                if 0 <= i - 1 < NTT:
                                    f2_b(i - 1)
                                if 0 <= i - 2 < NTT:
                                    f2_c(i - 2)
                            kb.barrier()
        kb.barrier()
    return nc


def _consts():
    cb = np.zeros((128, CB_N), np.float32)
    p = np.arange(128)
    cb[:, CB_ID:CB_ID + 128] = np.eye(128, dtype=np.float32)
    cb[:, CB_MTRI:CB_MTRI + 128] = (p[:, None] < p[None, :]).astype(np.float32)
    cb[:, CB_TRI:CB_TRI + 128] = -8.0 * (p[:, None] >= p[None, :]).astype(np.float32)
    for a in range(16):
        cb[:, CB_ES + a * 192 + 64 + a] = 1.0
        cb[:, CB_ES + a * 192 + 64 + 32 + a] = 1.0
    r = p % 64
    rb = np.where(r < 16, r, np.where((r >= 32) & (r < 48), r - 32, -1))
    blk = np.arange(2048) // 128
    cb[:, CB_L:CB_L + 2048] = -8.0 * ((rb[:, None] >= 0) & (blk[None, :] < rb[:, None])).astype(np.float32)
    cf = np.zeros((128, CF_N), np.float32)
    hi = (r < 32)
    cf[:, CF_SGN] = np.where(hi, 1.0, -1.0)
    cf[:, CF_MLO] = np.where(hi, 0.0, 1.0)
    cf[:, CF_NH] = -0.5
    cf[:, CF_EPS] = EPS
    for g, win in enumerate(POOL_WINDOWS):
        cf[:, CF_CNT + g * 16:CF_CNT + (g + 1) * 16] = 1.0 / np.minimum(np.arange(16) + 1, win).astype(np.float32)[None, :]
    return cb, cf


def make_in_maps(inputs, ncores=NCORES, nseq=NSEQ_CORE):
    f = lambda a: np.ascontiguousarray(np.asarray(a, dtype=np.float32))
    x = f(inputs["x"])
    c = f(inputs["c"])
    w_in = f(inputs["w_in"])[0].reshape(8, 128, 8192)
    Wp = f(w_in[:, :, :4096].reshape(8, 128, 4, 8, 128).transpose(3, 1, 0, 2, 4))
    Wpl = f(w_in[:, :, 4096:6144].reshape(8, 128, 2, 4, 2, 128).transpose(3, 1, 0, 2, 4, 5).reshape(4, 128, 8, 4, 128))
    Wm = f(w_in[:, :, 6144:8192].reshape(8, 128, 2, 8, 128).transpose(3, 1, 0, 2, 4))
    wa = f(inputs["w_branch_a"])[0].reshape(8, 128, 8, 128)
    wb = f(inputs["w_branch_b"])[0].reshape(8, 128, 8, 128)
    Wab = f(np.stack([wa, wb], axis=3).transpose(2, 1, 0, 3, 4))
    w_out_l = f(f(inputs["w_out"])[0].reshape(8, 128, 1024).transpose(1, 0, 2))
    w_ada_l = f(f(inputs["w_ada"])[0].reshape(8, 128, 3072).transpose(1, 0, 2))
    pool_w_l = f(f(inputs["pool_w"])[0].reshape(4, 2, 128, 256).transpose(2, 0, 1, 3))
    pool_scale_l = f(f(inputs["pool_scale"])[0].reshape(8, 128).T)
    cb, cf = _consts()
    shared = {
        "w_ada_l": w_ada_l, "b_ada": f(inputs["b_ada"]).reshape(1, 3072),
        "norm_gain": f(inputs["norm_gain"]).reshape(1, D), "final_gain": f(inputs["final_gain"]).reshape(1, D),
        "Wp": Wp, "Wpl": Wpl, "Wm": Wm, "Wab": Wab, "w_out_l": w_out_l, "pool_w_l": pool_w_l,
        "pool_scale_l": pool_scale_l, "constsB": cb, "constsF": cf,
    }
    maps = []
    for i in range(ncores):
        xs = x[i * nseq:(i + 1) * nseq].reshape(nseq * SEQ, D)
        cs = c[i * nseq:(i + 1) * nseq]
        cT = f(cs.reshape(nseq, 8, 128).transpose(2, 1, 0))
        m = dict(shared)
        m["x"] = f(xs)
        m["cT"] = cT
        maps.append(m)
    return maps


def kernel(**inputs):
    nc = build(NSEQ_CORE)
    in_maps = make_in_maps(inputs)
    res = run_bass_kernel_spmd(nc, in_maps, core_ids=list(range(NCORES)))
    outs = [np.asarray(r["out"], dtype=np.float32).reshape(NSEQ_CORE, SEQ, D) for r in res.results]
    return np.concatenate(outs, axis=0)
```

```python
import numpy as np
from contextlib import ExitStack
import concourse.bass as bass
import concourse.mybir as mybir
from concourse.bass_utils import run_bass_kernel_spmd

F32 = mybir.dt.float32
BF16 = mybir.dt.bfloat16
AF = mybir.ActivationFunctionType
ALU = mybir.AluOpType

NCORES = 8
SEQ = 2048
D = 1024
NSEQ_CORE = 4
EPS = 1e-6
NCH = 4
NTT = 16
POOL_WINDOWS = (2, 4, 8, 16)
USE_SOFTPLUS = True

CB_ID, CB_MTRI, CB_TRI, CB_ES, CB_L = 0, 128, 256, 384, 384 + 16 * 192
CB_N = CB_L + 2048
CF_SGN, CF_MLO, CF_CNT = 0, 1, 2
CF_NH = 2 + 64
CF_EPS = 2 + 64 + 1
CF_N = 2 + 64 + 2


class Buf:
    __slots__ = ("w", "r", "dsem", "dcnt", "uid")
    _n = 0

    def __init__(self):
        self.w = None
        self.r = {}
        self.dsem = None
        self.dcnt = 0
        Buf._n += 1
        self.uid = Buf._n


class KB:
    def __init__(self, nc, es):
        self.nc = nc
        self.es = es
        self.eng = {"pe": nc.tensor, "act": nc.scalar, "dve": nc.vector, "pool": nc.gpsimd, "sp": nc.sync}
        self.sem = {k: es.enter_context(nc.semaphore("s_" + k)) for k in self.eng}
        self.cnt = {k: 0 for k in self.eng}
        self.waited = {}
        self.dbufs = []
        self.free_sems = {True: [], False: []}
        self.nsem = 0
        self.nwaits = 0

    def _wait(self, e, tok):
        if tok is None:
            return
        key, val = tok
        if isinstance(key, str):
            if key == e and e == "pe":
                return
            k = (e, key)
            sem = self.sem[key]
        else:
            if key.dsem is None or key.dcnt < val:
                return
            k = (e, key.uid)
            sem = key.dsem
        if self.waited.get(k, 0) >= val:
            return
        self.eng[e].wait_ge(sem, val)
        self.nwaits += 1
        self.waited[k] = val

    def _deps(self, e, reads, writes):
        for b in reads:
            self._wait(e, b.w)
        for b in writes:
            self._wait(e, b.w)
            for t in b.r.values():
                self._wait(e, t)

    def op(self, e, fn, reads=(), writes=(), sig=True):
        self._deps(e, reads, writes)
        inst = fn(self.eng[e])
        if sig:
            self.cnt[e] += 1
            inst.then_inc(self.sem[e], 1)
            tok = (e, self.cnt[e])
        else:
            tok = (e, self.cnt[e] + 1)
        for b in reads:
            b.r[e] = tok
        for b in writes:
            b.w = tok
            b.r = {}
        return inst

    def dma(self, q, out, in_, sb, reads=(), writes=()):
        self._deps(q, reads, writes)
        if sb.dsem is None:
            pool_ = self.free_sems[q == "pool"]
            if pool_:
                sb.dsem, sb.dcnt = pool_.pop()
            else:
                self.nsem += 1
                sb.dsem = self.es.enter_context(self.nc.semaphore("d%d" % self.nsem))
                sb.dcnt = 0
            self.dbufs.append((sb, q == "pool"))
        inst = self.eng[q].dma_start(out=out, in_=in_)
        sb.dcnt += 16
        inst.then_inc(sb.dsem, 16)
        tok = (sb, sb.dcnt)
        for b in reads:
            b.r[("dma", sb.uid)] = tok
        for b in writes:
            b.w = tok
            b.r = {}

    def barrier(self):
        for e in self.eng:
            for f in ("pe", "act", "dve", "pool"):
                if f != e and self.cnt[f] > 0:
                    self._wait(e, (f, self.cnt[f]))
            for b, _sw in self.dbufs:
                self._wait(e, (b, b.dcnt))
        for b, sw in self.dbufs:
            self.free_sems[sw].append((b.dsem, b.dcnt))
            b.dsem = None
            b.dcnt = -1
        self.dbufs = []


def build(nseq=NSEQ_CORE):
    nc = bass.Bass("TRN2", target_bir_lowering=False)
    NTOK = nseq * SEQ

    def din(name, shape):
        return nc.dram_tensor(name, list(shape), F32, kind="ExternalInput").ap()

    x_d = din("x", [NTOK, D])
    cT_d = din("cT", [128, 8, nseq])
    wada_d = din("w_ada_l", [128, 8, 3072])
    bada_d = din("b_ada", [1, 3072])
    ngain_d = din("norm_gain", [1, D])
    fgain_d = din("final_gain", [1, D])
    wp_d = din("Wp", [8, 128, 8, 4, 128])
    wpl_d = din("Wpl", [4, 128, 8, 4, 128])
    wm_d = din("Wm", [8, 128, 8, 2, 128])
    wab_d = din("Wab", [8, 128, 8, 2, 128])
    wout_d = din("w_out_l", [128, 8, 1024])
    poolw_d = din("pool_w_l", [128, 4, 2, 256])
    pscale_d = din("pool_scale_l", [128, 8])
    cb_d = din("constsB", [128, CB_N])
    cf_d = din("constsF", [128, CF_N])
    out_d = nc.dram_tensor("out", [NTOK, D], F32, kind="ExternalOutput").ap()
    mod_d = nc.dram_tensor("mod_scratch", [nseq, 3072], F32, kind="Internal").ap()

    with ExitStack() as es:
        kb = KB(nc, es)
        uid = [0]

        def sb(name, shape, dt, stack):
            uid[0] += 1
            return stack.enter_context(nc.sbuf_tensor("%s_%d" % (name, uid[0]), list(shape), dt))

        def psum_banks(stack, n, nbf=0):
            uid[0] += 1
            ps = [stack.enter_context(nc.psum_tensor("ps%d_%d" % (i, uid[0]), [128, 512], F32)) for i in range(n)]
            pb = [stack.enter_context(nc.psum_tensor("pb%d_%d" % (i, uid[0]), [128, 8, 128], BF16)) for i in range(nbf)]
            return ps, [Buf() for _ in range(n)], pb, [Buf() for _ in range(nbf)]

        cB = sb("cB", [128, CB_N], BF16, es)
        cF = sb("cF", [128, CF_N], F32, es)
        pscale = sb("pscale", [128, 8], F32, es)
        hT = sb("hT", [128, 8, SEQ], BF16, es)

        b_cB, b_cF = Buf(), Buf()
        b_const = Buf()
        b_hT = [Buf() for _ in range(NCH)]
        b_mod = Buf()

        ident = cB[:, CB_ID:CB_ID + 128]
        mtri = cB[:, CB_MTRI:CB_MTRI + 128]
        trineg = cB[:, CB_TRI:CB_TRI + 128]

        def es_sel(a, odd):
            o = CB_ES + a * 192 + (64 if odd else 0)
            return cB[:, o:o + 128]

        kb.dma("pool", cB[:], cb_d[:, :], b_cB, writes=[b_cB])
        kb.dma("sp", cF[:], cf_d[:, :], b_cF, writes=[b_cF])
        kb.dma("sp", pscale[:], pscale_d[:, :], b_cF, writes=[b_cF])

        with ExitStack() as ph:
            PS, b_PS, _, _ = psum_banks(ph, 2)
            cTt = sb("cTt", [128, 8, nseq], F32, ph)
            scT = sb("scT", [128, 8, nseq], BF16, ph)
            wad = [sb("wad%d" % i, [128, 8, 1024], BF16, ph) for i in range(3)]
            bad = sb("bad", [nseq, 3072], F32, ph)
            modS = sb("modS", [nseq, 3072], F32, ph)
            b_c, b_sc, b_bad, b_modS = Buf(), Buf(), Buf(), Buf()
            b_wad = [Buf() for _ in range(3)]
            kb.dma("sp", cTt[:], cT_d[:, :, :], b_c, writes=[b_c])
            kb.dma("sp", bad[:], bada_d[0:1, :].partition_broadcast(nseq), b_bad, writes=[b_bad])
            for kind in range(3):
                kb.dma("pool", wad[kind][:], wada_d[:, :, kind * 1024:(kind + 1) * 1024], b_wad[kind], writes=[b_wad[kind]])
            kb.op("act", lambda e: e.activation(out=scT[:], in_=cTt[:], func=AF.Silu), reads=[b_c], writes=[b_sc])
            for kind in range(3):
                for half in range(2):
                    pb = (kind * 2 + half) % 2
                    pst = PS[pb]
                    for kt in range(8):
                        kb.op("pe", lambda e: e.matmul(pst[0:nseq, :], scT[:, kt, :], wad[kind][:, kt, half * 512:(half + 1) * 512],
                                                       start=(kt == 0), stop=(kt == 7)),
                              reads=[b_sc, b_wad[kind]], writes=[b_PS[pb]], sig=(kt == 7))
                    col = kind * 1024 + half * 512
                    kb.op("dve", lambda e: e.tensor_tensor(out=modS[:, col:col + 512], in0=pst[0:nseq, :], in1=bad[:, col:col + 512], op=ALU.add),
                          reads=[b_PS[pb], b_bad], writes=[b_modS])
            kb.dma("sp", mod_d[:, :], modS[:], b_modS, reads=[b_modS], writes=[b_mod])
            kb.barrier()

        for bl in range(nseq):
            tok0 = bl * SEQ
            with ExitStack() as ph:
                _, _, PB, b_PB = psum_banks(ph, 0, 2)
                G = sb("G", [128, D], F32, ph)
                gain_bc = sb("gain_bc", [128, D], F32, ph)
                shift = sb("shift", [128, D], F32, ph)
                sctmp = sb("sctmp", [128, D], F32, ph)
                xt = [sb("xt%d" % i, [128, D], F32, ph) for i in range(4)]
                junk = sb("junk", [128, D], BF16, ph)
                t1 = [sb("t1%d" % i, [128, D], F32, ph) for i in range(2)]
                hb = [sb("hb%d" % i, [128, D], BF16, ph) for i in range(2)]
                ss = sb("ss", [128, NTT], F32, ph)
                rstd = sb("rstd", [128, NTT], F32, ph)
                b_G, b_shift, b_sct, b_gain, b_junk = Buf(), Buf(), Buf(), Buf(), Buf()
                b_xt = [Buf() for _ in range(4)]
                b_t1 = [Buf(), Buf()]
                b_hb = [Buf(), Buf()]
                b_ss = [Buf() for _ in range(NTT)]
                b_rstd = [Buf() for _ in range(NTT)]
                kb.dma("sp", gain_bc[:], ngain_d[0:1, :].partition_broadcast(128), b_gain, writes=[b_gain])
                kb.dma("sp", shift[:], mod_d[bl:bl + 1, 0:1024].partition_broadcast(128), b_shift, reads=[b_mod], writes=[b_shift])
                kb.dma("sp", sctmp[:], mod_d[bl:bl + 1, 1024:2048].partition_broadcast(128), b_sct, reads=[b_mod], writes=[b_sct])
                kb.op("dve", lambda e: e.scalar_tensor_tensor(out=G[:], in0=sctmp[:], scalar=1.0, in1=gain_bc[:], op0=ALU.add, op1=ALU.mult),
                      reads=[b_sct, b_gain], writes=[b_G])
                def x_stage1(tt):
                    s = tt % 4
                    kb.dma("sp", xt[s][:], x_d[tok0 + tt * 128: tok0 + (tt + 1) * 128, :], b_xt[s], writes=[b_xt[s]])
                    kb.op("act", lambda e: e.activation(out=junk[:], in_=xt[s][:], func=AF.Square, accum_out=ss[:, tt:tt + 1]),
                          reads=[b_xt[s]], writes=[b_junk, b_ss[tt]])
                    kb.op("act", lambda e: e.activation(out=rstd[:, tt:tt + 1], in_=ss[:, tt:tt + 1], func=AF.Ln, bias=cF[:, CF_EPS:CF_EPS + 1], scale=1.0 / D),
                          reads=[b_ss[tt]], writes=[b_rstd[tt]])
                    kb.op("act", lambda e: e.activation(out=rstd[:, tt:tt + 1], in_=rstd[:, tt:tt + 1], func=AF.Exp, scale=-0.5),
                          reads=[b_rstd[tt]], writes=[b_rstd[tt]])

                def x_stage2(tt):
                    s = tt % 2
                    s4 = tt % 4
                    kb.op("dve", lambda e: e.scalar_tensor_tensor(out=t1[s][:], in0=xt[s4][:], scalar=rstd[:, tt:tt + 1], in1=G[:],
                                                                  op0=ALU.mult, op1=ALU.mult),
                          reads=[b_xt[s4], b_rstd[tt], b_G], writes=[b_t1[s]])
                    kb.op("pool", lambda e: e.tensor_tensor(out=hb[s][:], in0=t1[s][:], in1=shift[:], op=ALU.add),
                          reads=[b_t1[s], b_shift], writes=[b_hb[s]])

                def x_stage3(tt):
                    s = tt % 2
                    for kt in range(8):
                        kb.op("pe", lambda e: e.transpose(PB[s][:, kt, :], hb[s][:, kt * 128:(kt + 1) * 128], ident),
                              reads=[b_hb[s]], writes=[b_PB[s]], sig=(kt == 7))
                    kb.op("act", lambda e: e.activation(out=hT[:, :, tt * 128:(tt + 1) * 128], in_=PB[s][:, :, :], func=AF.Copy),
                          reads=[b_PB[s]], writes=[b_hT[tt // 4]])

                for i in range(NTT + 2):
                    if i < NTT:
                        x_stage1(i)
                    if 0 <= i - 1 < NTT:
                        x_stage2(i - 1)
                    if 0 <= i - 2 < NTT:
                        x_stage3(i - 2)
                kb.barrier()

            with ExitStack() as phga:
                ga = sb("ga", [128, 8, SEQ], BF16, phga)
                b_ga = [[Buf() for _ in range(NCH)] for _ in range(8)]
                with ExitStack() as ph:
                    PS, b_PS, _, _ = psum_banks(ph, 8)
                    PB = PS[7][:].bitcast(BF16).rearrange("p (k c) -> p k c", k=8)
                    b_PB = b_PS[7]
                    NSL = 2
                    wp = [sb("wp%d" % i, [128, 8, 4, 128], BF16, ph) for i in range(NSL)]
                    qa = [[sb("qa%d_%d" % (i, h), [128, SEQ], BF16, ph) for h in range(2)] for i in range(NSL)]
                    ka = [[sb("ka%d_%d" % (i, h), [128, SEQ], BF16, ph) for h in range(2)] for i in range(NSL)]
                    zg = [sb("zg%d" % i, [128, SEQ], F32, ph) for i in range(NSL)]
                    vT = sb("vT", [128, SEQ], BF16, ph)
                    vt = [sb("vt%d" % i, [128, NTT, 128], BF16, ph) for i in range(NSL)]
                    sp = [sb("sp%d" % h, [128, 16, 512], BF16, ph) for h in range(2)]
                    NW = 4
                    wt = [sb("wt%d" % i, [128, 512], BF16, ph) for i in range(NW)]
                    HS = sb("HS", [128, 512], BF16, ph)
                    etmp = [] if USE_SOFTPLUS else [sb("etmp%d" % i, [128, 512], F32, ph) for i in range(3)]
                    b_etmp = [Buf() for _ in range(3)]
                    b_wp = [Buf() for _ in range(NSL)]
                    b_qa = [[[Buf() for _ in range(NCH)] for _ in range(2)] for _ in range(NSL)]
                    b_ka = [[[Buf() for _ in range(NCH)] for _ in range(2)] for _ in range(NSL)]
                    b_zg = [[Buf() for _ in range(NCH)] for _ in range(NSL)]
                    b_vT = [Buf() for _ in range(NCH)]
                    b_vt = [[Buf(), Buf()] for _ in range(NSL)]
                    b_sp = [[Buf() for _ in range(16)] for _ in range(2)]
                    b_wt = [Buf() for _ in range(NW)]
                    b_HS = Buf()
                    kb.dma("pool", wp[0][:], wp_d[0], b_wp[0], writes=[b_wp[0]])
                    kb.dma("pool", wp[1][:], wp_d[1], b_wp[1], writes=[b_wp[1]])
                    for i in range(NSL):
                        kb.dma("pool", ka[i][0][64:128, :], cb_d[64:128, CB_L:CB_L + 2048], Buf(), writes=[b_ka[i][0][c] for c in range(NCH)])
                        kb.dma("pool", ka[i][1][0:64, :], cb_d[0:64, CB_L:CB_L + 2048], Buf(), writes=[b_ka[i][1][c] for c in range(NCH)])
                    state = {"ww": 0, "pp": 0}
                    XB = [1, 2, 3]
                    PROJB = [0, 7]

                    def proj_pair(j, overlapped):
                        sl = j % NSL
                        kb.op("pool", lambda e: e.memset(qa[sl][0][64:128, :], 0.0), writes=[b_qa[sl][0][c] for c in range(NCH)])
                        kb.op("pool", lambda e: e.memset(qa[sl][1][0:64, :], 0.0), writes=[b_qa[sl][1][c] for c in range(NCH)])
                        for g in range(4):
                            for c in range(NCH):
                                pi = PROJB[state["pp"] % 2]
                                state["pp"] += 1
                                pst = PS[pi]
                                for kt in range(8):
                                    kb.op("pe", lambda e: e.matmul(pst[:], wp[sl][:, kt, g, :], hT[:, kt, c * 512:(c + 1) * 512],
                                                                   start=(kt == 0), stop=(kt == 7)),
                                          reads=[b_wp[sl], b_hT[c]], writes=[b_PS[pi]], sig=(kt == 7))
                                    if overlapped and kt == 3:
                                        yield
                                cs = slice(c * 512, (c + 1) * 512)
                                if g == 0 or g == 1:
                                    T, bT = (qa, b_qa) if g == 0 else (ka, b_ka)
                                    kb.op("dve", lambda e: e.tensor_copy(out=T[sl][0][0:64, cs], in_=pst[0:64, :]),
                                          reads=[b_PS[pi]], writes=[bT[sl][0][c]])
                                    kb.op("dve", lambda e: e.tensor_copy(out=T[sl][1][64:128, cs], in_=pst[64:128, :]),
                                          reads=[b_PS[pi]], writes=[bT[sl][1][c]])
                                elif g == 2:
                                    kb.op("dve", lambda e: e.tensor_copy(out=vT[:, cs], in_=pst[:]),
                                          reads=[b_PS[pi]], writes=[b_vT[c]])
                                else:
                                    kb.op("dve", lambda e: e.tensor_copy(out=zg[sl][:, cs], in_=pst[:]),
                                          reads=[b_PS[pi]], writes=[b_zg[sl][c]])
                                yield
                        for hf in range(2):
                            for k8 in range(8):
                                kbi = hf * 8 + k8
                                kb.op("pe", lambda e: e.transpose(PB[:, k8, :], vT[:, kbi * 128:(kbi + 1) * 128], ident),
                                      reads=[b_vT[kbi // 4]], writes=[b_PB], sig=(k8 == 7))
                            kb.op("dve", lambda e: e.tensor_copy(out=vt[sl][:, hf * 8:(hf + 1) * 8, :], in_=PB[:, :, :]),
                                  reads=[b_PB], writes=[b_vt[sl][hf]])
                            yield

                    def silu_zg(sl):
                        kb.op("act", lambda e: e.activation(out=zg[sl][:, :], in_=zg[sl][:, :], func=AF.Silu),
                              reads=[b_zg[sl][c] for c in range(NCH)], writes=[b_zg[sl][c] for c in range(NCH)])

                    def attn_pair(j):
                        sl = j % NSL
                        SK = 2
                        for c in range(NCH):
                            n = 4 * c + 4
                            cbase = c * 512
                            blocks = [(hd, a) for a in range(n) for hd in range(2)]
                            NB = len(blocks)

                            def lo_of(a):
                                return max(0, a - 4 * c) * 128

                            def p1_front(i):
                                hd, a = blocks[i]
                                lo = lo_of(a)
                                xi = XB[i % len(XB)]
                                X = PS[xi]
                                kb.op("pe", lambda e: e.matmul(X[:, lo:512], ka[sl][hd][:, a * 128:(a + 1) * 128],
                                                               qa[sl][hd][:, cbase + lo:cbase + 512], start=True, stop=True),
                                      reads=[b_ka[sl][hd][a // 4], b_qa[sl][hd][c]], writes=[b_PS[xi]])
                                if USE_SOFTPLUS:
                                    kb.op("act", lambda e: e.activation(out=sp[hd][:, a, lo:512], in_=X[:, lo:512], func=AF.Softplus, scale=0.125),
                                          reads=[b_PS[xi]], writes=[b_sp[hd][a]])
                                else:
                                    ei = i % 3
                                    kb.op("act", lambda e: e.activation(out=etmp[ei][:, lo:512], in_=X[:, lo:512], func=AF.Exp, scale=0.125),
                                          reads=[b_PS[xi]], writes=[b_etmp[ei]])
                                    kb.op("act", lambda e: e.activation(out=sp[hd][:, a, lo:512], in_=etmp[ei][:, lo:512], func=AF.Ln, bias=1.0, scale=1.0),
                                          reads=[b_etmp[ei]], writes=[b_sp[hd][a]])
                                if a >= 4 * c:
                                    kb.op("dve", lambda e: e.tensor_tensor(out=sp[hd][:, a, lo:lo + 128], in0=sp[hd][:, a, lo:lo + 128],
                                                                           in1=mtri, op=ALU.mult),
                                          reads=[b_sp[hd][a]], writes=[b_sp[hd][a]])

                            def p1_back(i):
                                hd, a = blocks[i]
                                lo = lo_of(a)
                                kb.op("pe", lambda e: e.matmul(PS[4][:, lo:512], es_sel(a, hd == 1), sp[hd][:, a, lo:512],
                                                               start=(i == 0), stop=(i == NB - 1)),
                                      reads=[b_sp[hd][a]], writes=[b_PS[4]], sig=(i == NB - 1))

                            for i in range(NB + SK):
                                if i < NB:
                                    p1_front(i)
                                if i - SK >= 0:
                                    p1_back(i - SK)
                                yield "step1"
                            kb.op("dve", lambda e: e.tensor_scalar(out=HS[:], in0=PS[4][:], scalar1=cF[:, CF_SGN:CF_SGN + 1], scalar2=None,
                                                                   op0=ALU.mult), reads=[b_PS[4]], writes=[b_HS])
                            kb.op("dve", lambda e: e.scalar_tensor_tensor(out=qa[sl][0][64:128, cbase:cbase + 512], in0=PS[4][64:128, :],
                                                                          scalar=cF[64:128, CF_MLO:CF_MLO + 1], in1=HS[64:128, :],
                                                                          op0=ALU.mult, op1=ALU.add),
                                  reads=[b_PS[4], b_HS], writes=[b_qa[sl][0][c]])
                            kb.op("dve", lambda e: e.scalar_tensor_tensor(out=qa[sl][1][0:64, cbase:cbase + 512], in0=PS[4][0:64, :],
                                                                          scalar=cF[0:64, CF_MLO:CF_MLO + 1], in1=HS[0:64, :],
                                                                          op0=ALU.mult, op1=ALU.add),
                                  reads=[b_PS[4], b_HS], writes=[b_qa[sl][1][c]])
                            if c == 0:
                                silu_zg(sl)
                            yield "drain"
                            wslot = {}

                            def p3_front(i):
                                hd, a = blocks[i]
                                lo = lo_of(a)
                                xi = XB[i % len(XB)]
                                X = PS[xi]
                                kb.op("pe", lambda e: e.matmul(X[:, lo:512], ka[sl][hd][:, a * 128:(a + 1) * 128],
                                                               qa[sl][hd][:, cbase + lo:cbase + 512], start=True, stop=False),
                                      reads=[b_ka[sl][hd][a // 4], b_qa[sl][hd][c]], writes=[b_PS[xi]], sig=False)
                                kb.op("pe", lambda e: e.matmul(X[:, lo:512], trineg, sp[hd][:, a, lo:512], start=False, stop=True),
                                      reads=[b_sp[hd][a]], writes=[b_PS[xi]])
                                wi = state["ww"] % NW
                                state["ww"] += 1
                                wslot[i] = wi
                                kb.op("act", lambda e: e.activation(out=wt[wi][:, lo:512], in_=X[:, lo:512], func=AF.Exp, scale=0.125),
                                      reads=[b_PS[xi]], writes=[b_wt[wi]])
                                if a >= 4 * c:
                                    kb.op("dve", lambda e: e.tensor_tensor(out=wt[wi][:, lo:lo + 128], in0=wt[wi][:, lo:lo + 128],
                                                                           in1=mtri, op=ALU.mult),
                                          reads=[b_wt[wi]], writes=[b_wt[wi]])

                            def p3_back(i):
                                hd, a = blocks[i]
                                lo = lo_of(a)
                                wi = wslot[i]
                                oi = 5 + hd
                                rows = slice(0, 64) if hd == 0 else slice(64, 128)
                                kb.op("pe", lambda e: e.matmul(PS[oi][:, lo:512], vt[sl][:, a, :], wt[wi][:, lo:512],
                                                               start=(a == 0), stop=(a == n - 1)),
                                      reads=[b_vt[sl][a // 8], b_wt[wi]], writes=[b_PS[oi]], sig=(a == n - 1))
                                if a == n - 1:
                                    kb.op("dve", lambda e: e.tensor_tensor(out=ga[rows, j, cbase:cbase + 512], in0=PS[oi][rows, :],
                                                                           in1=zg[sl][rows, cbase:cbase + 512], op=ALU.mult),
                                          reads=[b_PS[oi], b_zg[sl][c]], writes=[b_ga[j][c]])

                            for i in range(NB + SK):
                                if i < NB:
                                    p3_front(i)
                                if i - SK >= 0:
                                    p3_back(i - SK)
                                yield "step"
                            yield "drain"

                    def run_interleaved(gmain, gside, ratio, ndrain):
                        k = 0
                        side_done = gside is None

                        def side(nsteps):
                            nonlocal side_done
                            for _ in range(nsteps):
                                if side_done:
                                    return
                                try:
                                    next(gside)
                                except StopIteration:
                                    side_done = True

                        for tag in gmain:
                            if tag == "drain":
                                side(ndrain)
                            elif tag == "step1":
                                k += 1
                                if k % ratio == 0:
                                    side(1)
                        if not side_done:
                            for _ in gside:
                                pass

                    for _ in proj_pair(0, False):
                        pass
                    for j in range(8):
                        if j + 2 < 8:
                            kb.dma("pool", wp[j % NSL][:], wp_d[j + 2], b_wp[j % NSL], writes=[b_wp[j % NSL]])
                        nxt = proj_pair(j + 1, True) if j + 1 < 8 else None
                        run_interleaved(attn_pair(j), nxt, 1000, 5)
                    kb.barrier()

                with ExitStack() as phgb:
                    gb = sb("gb", [128, 8, SEQ], BF16, phgb)
                    b_gb = [[Buf() for _ in range(NCH)] for _ in range(8)]
                    with ExitStack() as ph:
                        PS, b_PS, _, _ = psum_banks(ph, 6)
                        wg = [sb("wg%d" % i, [128, 8, 4, 128], BF16, ph) for i in range(2)]
                        poolw = sb("poolw", [128, 4, 2, 256], BF16, ph)
                        U = [[sb("U%d_%d" % (s_, ci), [128, 16 + 512], F32, ph) for ci in range(2)] for s_ in range(2)]
                        SA = [sb("SA%d" % ci, [128, 16 + 512], F32, ph) for ci in range(2)]
                        SB_ = [sb("SB%d" % ci, [128, 16 + 512], F32, ph) for ci in range(2)]
                        dT = [[sb("dT%d_%d" % (s_, ci), [128, 512], BF16, ph) for ci in range(2)] for s_ in range(2)]
                        zbs = [[sb("zbs%d_%d" % (s_, dt_), [128, 512], F32, ph) for dt_ in range(2)] for s_ in range(2)]
                        fx = sb("fx", [128, 16], F32, ph)
                        b_wg = [Buf(), Buf()]
                        b_poolw, b_fx = Buf(), Buf()
                        b_U = [[Buf(), Buf()], [Buf(), Buf()]]
                        b_SA = [Buf(), Buf()]
                        b_SB = [Buf(), Buf()]
                        b_dT = [[Buf(), Buf()], [Buf(), Buf()]]
                        b_zbs = [[Buf(), Buf()], [Buf(), Buf()]]
                        kb.dma("pool", poolw[:], poolw_d[:, :, :, :], b_poolw, writes=[b_poolw])
                        state_p = {"pp": 0}

                        def p_stage_a(g, c, s_):
                            win = POOL_WINDOWS[g]
                            gs = g % 2
                            cs = slice(c * 512, (c + 1) * 512)
                            if c == 0:
                                if g == 0:
                                    kb.dma("pool", wg[0][:], wpl_d[0], b_wg[0], writes=[b_wg[0]])
                                if g + 1 < 4:
                                    kb.dma("pool", wg[(g + 1) % 2][:], wpl_d[g + 1], b_wg[(g + 1) % 2], writes=[b_wg[(g + 1) % 2]])
                                for ci in range(2):
                                    kb.op("pool", lambda e: e.memset(U[s_][ci][:, 0:16], 0.0), writes=[b_U[s_][ci]])
                            for which in range(4):
                                pi = state_p["pp"] % 4
                                state_p["pp"] += 1
                                pst = PS[pi]
                                for kt in range(8):
                                    kb.op("pe", lambda e: e.matmul(pst[:], wg[gs][:, kt, which, :], hT[:, kt, cs],
                                                                   start=(kt == 0), stop=(kt == 7)),
                                          reads=[b_wg[gs], b_hT[c]], writes=[b_PS[pi]], sig=(kt == 7))
                                if which < 2:
                                    ci = which
                                    kb.op("act", lambda e: e.activation(out=U[s_][ci][:, 16:528], in_=pst[:], func=AF.Copy),
                                          reads=[b_PS[pi]], writes=[b_U[s_][ci]])
                                else:
                                    dt_ = which - 2
                                    kb.op("act", lambda e: e.activation(out=zbs[s_][dt_][:], in_=pst[:], func=AF.Silu),
                                          reads=[b_PS[pi]], writes=[b_zbs[s_][dt_]])
                            for ci in range(2):
                                eng = "pool" if ci == 0 else "dve"
                                u = U[s_][ci]
                                bu = b_U[s_][ci]
                                if c + 1 < NCH:
                                    kb.op("pool", lambda e: e.tensor_copy(out=U[1 - s_][ci][:, 0:16], in_=u[:, 512:528]),
                                          reads=[bu], writes=[b_U[1 - s_][ci]])
                                cur, bcur = u, bu
                                nxts = [(SA[ci], b_SA[ci]), (SB_[ci], b_SB[ci])]
                                first = 0
                                for step in range(g + 1):
                                    sh = 1 << step
                                    nx, bnx = nxts[step % 2]
                                    f0 = first + sh
                                    kb.op(eng, lambda e: e.tensor_tensor(out=nx[:, f0:528], in0=cur[:, f0:528],
                                                                         in1=cur[:, f0 - sh:528 - sh], op=ALU.add),
                                          reads=[bcur], writes=[bnx])
                                    cur, bcur = nx, bnx
                                    first = f0
                                kb.op("dve", lambda e: e.scalar_tensor_tensor(out=dT[s_][ci][:], in0=cur[:, 16:528], scalar=1.0 / win,
                                                                              in1=u[:, 16:528], op0=ALU.mult, op1=ALU.subtract),
                                      reads=[bcur, bu], writes=[b_dT[s_][ci]])
                                if c == 0:
                                    kb.op("dve", lambda e: e.tensor_tensor(out=fx[:], in0=cur[:, 16:32],
                                                                           in1=cF[:, CF_CNT + g * 16:CF_CNT + (g + 1) * 16], op=ALU.mult),
                                          reads=[bcur], writes=[b_fx])
                                    kb.op("dve", lambda e: e.tensor_tensor(out=dT[s_][ci][:, 0:16], in0=fx[:], in1=u[:, 16:32], op=ALU.subtract),
                                          reads=[b_fx, bu, b_dT[s_][ci]], writes=[b_dT[s_][ci]])

                        def p_stage_b(g, c, s_):
                            cs = slice(c * 512, (c + 1) * 512)
                            for dt_ in range(2):
                                ft = 2 * g + dt_
                                pi = 4 + dt_
                                pst = PS[pi]
                                for ci in range(2):
                                    kb.op("pe", lambda e: e.matmul(pst[:], poolw[:, g, ci, dt_ * 128:(dt_ + 1) * 128], dT[s_][ci][:],
                                                                   start=(ci == 0), stop=(ci == 1)),
                                          reads=[b_poolw, b_dT[s_][ci]], writes=[b_PS[pi]], sig=(ci == 1))
                                kb.op("dve", lambda e: e.scalar_tensor_tensor(out=gb[:, ft, cs], in0=pst[:],
                                                                              scalar=pscale[:, ft:ft + 1], in1=zbs[s_][dt_][:],
                                                                              op0=ALU.mult, op1=ALU.mult),
                                      reads=[b_PS[pi], b_zbs[s_][dt_]], writes=[b_gb[ft][c]])

                        steps = [(g, c) for g in range(4) for c in range(NCH)]
                        for k in range(len(steps) + 1):
                            if k < len(steps):
                                p_stage_a(steps[k][0], steps[k][1], k % 2)
                            if k - 1 >= 0:
                                p_stage_b(steps[k - 1][0], steps[k - 1][1], (k - 1) % 2)
                        kb.barrier()

                    with ExitStack() as phf:
                        mg = sb("mg", [128, 8, SEQ], BF16, phf)
                        wout = sb("wout", [128, 8, 1024], BF16, phf)
                        b_mg = [[Buf() for _ in range(NCH)] for _ in range(8)]
                        b_wout = Buf()
                        kb.dma("pool", wout[:], wout_d[:, :, :], b_wout, writes=[b_wout])
                        with ExitStack() as ph:
                            PS, b_PS, _, _ = psum_banks(ph, 8)
                            NS3 = 3
                            wab = [sb("wab%d" % i, [128, 8, 2, 128], BF16, ph) for i in range(NS3)]
                            wm = [sb("wm%d" % i, [128, 8, 2, 128], BF16, ph) for i in range(NS3)]
                            sgA = [sb("sgA%d" % i, [128, 512], F32, ph) for i in range(2)]
                            sgB = [sb("sgB%d" % i, [128, 512], F32, ph) for i in range(2)]
                            tA = [sb("tA%d" % i, [128, 512], F32, ph) for i in range(2)]
                            tB = [sb("tB%d" % i, [128, 512], F32, ph) for i in range(2)]
                            b_wab = [Buf() for _ in range(NS3)]
                            b_wm = [Buf() for _ in range(NS3)]
                            b_sgA, b_sgB, b_tA, b_tB = [Buf(), Buf()], [Buf(), Buf()], [Buf(), Buf()], [Buf(), Buf()]

                            def load_m(m):
                                s3 = m % NS3
                                kb.dma("pool", wab[s3][:], wab_d[m], b_wab[s3], writes=[b_wab[s3]])
                                kb.dma("pool", wm[s3][:], wm_d[m], b_wm[s3], writes=[b_wm[s3]])

                            load_m(0)
                            load_m(1)
                            step = 0
                            for m in range(8):
                                s3 = m % NS3
                                if m + 2 < 8:
                                    load_m(m + 2)
                                for c in range(NCH):
                                    cs = slice(c * 512, (c + 1) * 512)
                                    st2 = step % 2
                                    step += 1
                                    pA, pB_, pMa, pMb = [4 * st2 + k_ for k_ in range(4)]
                                    for kt in range(8):
                                        kb.op("pe", lambda e: e.matmul(PS[pMa][:], wm[s3][:, kt, 0, :], hT[:, kt, cs], start=(kt == 0), stop=(kt == 7)),
                                              reads=[b_wm[s3], b_hT[c]], writes=[b_PS[pMa]], sig=(kt == 7))
                                    for kt in range(8):
                                        kb.op("pe", lambda e: e.matmul(PS[pMb][:], wm[s3][:, kt, 1, :], hT[:, kt, cs], start=(kt == 0), stop=(kt == 7)),
                                              reads=[b_wm[s3], b_hT[c]], writes=[b_PS[pMb]], sig=(kt == 7))
                                    for kt in range(8):
                                        kb.op("pe", lambda e: e.matmul(PS[pA][:], wab[s3][:, kt, 0, :], ga[:, kt, cs], start=(kt == 0), stop=(kt == 7)),
                                              reads=[b_wab[s3], b_ga[kt][c]], writes=[b_PS[pA]], sig=(kt == 7))
                                    for kt in range(8):
                                        kb.op("pe", lambda e: e.matmul(PS[pB_][:], wab[s3][:, kt, 1, :], gb[:, kt, cs], start=(kt == 0), stop=(kt == 7)),
                                              reads=[b_wab[s3], b_gb[kt][c]], writes=[b_PS[pB_]], sig=(kt == 7))
                                    kb.op("act", lambda e: e.activation(out=sgA[st2][:], in_=PS[pMa][:], func=AF.Sigmoid),
                                          reads=[b_PS[pMa]], writes=[b_sgA[st2]])
                                    kb.op("act", lambda e: e.activation(out=sgB[st2][:], in_=PS[pMb][:], func=AF.Sigmoid),
                                          reads=[b_PS[pMb]], writes=[b_sgB[st2]])
                                    kb.op("dve", lambda e: e.tensor_tensor(out=tA[st2][:], in0=PS[pA][:], in1=sgA[st2][:], op=ALU.mult),
                                          reads=[b_PS[pA], b_sgA[st2]], writes=[b_tA[st2]])
                                    kb.op("dve", lambda e: e.tensor_tensor(out=tB[st2][:], in0=PS[pB_][:], in1=sgB[st2][:], op=ALU.mult),
                                          reads=[b_PS[pB_], b_sgB[st2]], writes=[b_tB[st2]])
                                    kb.op("pool", lambda e: e.tensor_tensor(out=mg[:, m, cs], in0=tA[st2][:], in1=tB[st2][:], op=ALU.add),
                                          reads=[b_tA[st2], b_tB[st2]], writes=[b_mg[m][c]])
                            kb.barrier()
                        with ExitStack() as ph:
                            PS, b_PS, _, _ = psum_banks(ph, 4)
                            fgain_bc = sb("fgain_bc", [128, D], F32, ph)
                            gate_bc = sb("gate_bc", [128, D], F32, ph)
                            xr = [sb("xr%d" % i, [128, D], F32, ph) for i in range(4)]
                            yy = [sb("yy%d" % i, [128, D], F32, ph) for i in range(5)]
                            junk2 = sb("junk2", [128, D], BF16, ph)
                            ss2 = sb("ss2", [128, NTT], F32, ph)
                            rs2 = sb("rs2", [128, NTT], F32, ph)
                            b_gain, b_gate, b_junk2 = Buf(), Buf(), Buf()
                            b_xr = [Buf() for _ in range(4)]
                            b_yy = [Buf() for _ in range(5)]
                            b_ss2 = [Buf() for _ in range(NTT)]
                            b_rs2 = [Buf() for _ in range(NTT)]
                            kb.dma("sp", fgain_bc[:], fgain_d[0:1, :].partition_broadcast(128), b_gain, writes=[b_gain])
                            kb.dma("sp", gate_bc[:], mod_d[bl:bl + 1, 2048:3072].partition_broadcast(128), b_gate, reads=[b_mod], writes=[b_gate])
                            NY = 5

                            def f2_load(tt):
                                s = tt % 4
                                r0 = tok0 + tt * 128
                                kb.dma("sp", xr[s][:], x_d[r0:r0 + 128, :], b_xr[s], writes=[b_xr[s]])

                            def f2_a(tt):
                                c = tt // 4
                                s = tt % 4
                                y3 = tt % NY
                                if tt == 0:
                                    for t0_ in range(3):
                                        f2_load(t0_)
                                if tt + 3 < NTT:
                                    f2_load(tt + 3)
                                for half in range(2):
                                    pi = 2 * (tt % 2) + half
                                    for m in range(8):
                                        kb.op("pe", lambda e: e.matmul(PS[pi][:], mg[:, m, tt * 128:(tt + 1) * 128], wout[:, m, half * 512:(half + 1) * 512],
                                                                       start=(m == 0), stop=(m == 7)),
                                              reads=[b_mg[m][c], b_wout], writes=[b_PS[pi]], sig=(m == 7))
                                    hs = slice(half * 512, (half + 1) * 512)
                                    kb.op("dve", lambda e: e.tensor_tensor(out=yy[y3][:, hs], in0=PS[pi][:], in1=gate_bc[:, hs], op=ALU.mult),
                                          reads=[b_PS[pi], b_gate], writes=[b_yy[y3]])
                                kb.op("pool", lambda e: e.tensor_tensor(out=yy[y3][:], in0=yy[y3][:], in1=xr[s][:], op=ALU.add),
                                      reads=[b_yy[y3], b_xr[s]], writes=[b_yy[y3]])
                                kb.op("act", lambda e: e.activation(out=junk2[:], in_=yy[y3][:], func=AF.Square, accum_out=ss2[:, tt:tt + 1]),
                                      reads=[b_yy[y3]], writes=[b_junk2, b_ss2[tt]])

                            def f2_b(tt):
                                kb.op("act", lambda e: e.activation(out=rs2[:, tt:tt + 1], in_=ss2[:, tt:tt + 1], func=AF.Ln, bias=cF[:, CF_EPS:CF_EPS + 1], scale=1.0 / D),
                                      reads=[b_ss2[tt]], writes=[b_rs2[tt]])
                                kb.op("act", lambda e: e.activation(out=rs2[:, tt:tt + 1], in_=rs2[:, tt:tt + 1], func=AF.Exp, scale=-0.5),
                                      reads=[b_rs2[tt]], writes=[b_rs2[tt]])
                                y3 = tt % NY
                                kb.op("act", lambda e: e.activation(out=yy[y3][:], in_=yy[y3][:], func=AF.Identity, scale=rs2[:, tt:tt + 1]),
                                      reads=[b_yy[y3], b_rs2[tt]], writes=[b_yy[y3]])

                            def f2_c(tt):
                                y3 = tt % NY
                                r0 = tok0 + tt * 128
                                kb.op("dve", lambda e: e.tensor_tensor(out=yy[y3][:], in0=yy[y3][:], in1=fgain_bc[:], op=ALU.mult),
                                      reads=[b_yy[y3], b_gain], writes=[b_yy[y3]])
                                kb.dma("sp", out_d[r0:r0 + 128, :], yy[y3][:], b_yy[y3], reads=[b_yy[y3]])

                            for i in range(NTT + 2):
                                if i < NTT:
                                    f2_a(i)
                                if 0 <= i - 1 < NTT:
                                    f2_b(i - 1)
                                if 0 <= i - 2 < NTT:
                                    f2_c(i - 2)
                            kb.barrier()
        kb.barrier()
    return nc


def _consts():
    cb = np.zeros((128, CB_N), np.float32)
    p = np.arange(128)
    cb[:, CB_ID:CB_ID + 128] = np.eye(128, dtype=np.float32)
    cb[:, CB_MTRI:CB_MTRI + 128] = (p[:, None] < p[None, :]).astype(np.float32)
    cb[:, CB_TRI:CB_TRI + 128] = -8.0 * (p[:, None] >= p[None, :]).astype(np.float32)
    for a in range(16):
        cb[:, CB_ES + a * 192 + 64 + a] = 1.0
        cb[:, CB_ES + a * 192 + 64 + 32 + a] = 1.0
    r = p % 64
    rb = np.where(r < 16, r, np.where((r >= 32) & (r < 48), r - 32, -1))
    blk = np.arange(2048) // 128
    cb[:, CB_L:CB_L + 2048] = -8.0 * ((rb[:, None] >= 0) & (blk[None, :] < rb[:, None])).astype(np.float32)
    cf = np.zeros((128, CF_N), np.float32)
    hi = (r < 32)
    cf[:, CF_SGN] = np.where(hi, 1.0, -1.0)
    cf[:, CF_MLO] = np.where(hi, 0.0, 1.0)
    cf[:, CF_NH] = -0.5
    cf[:, CF_EPS] = EPS
    for g, win in enumerate(POOL_WINDOWS):
        cf[:, CF_CNT + g * 16:CF_CNT + (g + 1) * 16] = 1.0 / np.minimum(np.arange(16) + 1, win).astype(np.float32)[None, :]
    return cb, cf


def make_in_maps(inputs, ncores=NCORES, nseq=NSEQ_CORE):
    f = lambda a: np.ascontiguousarray(np.asarray(a, dtype=np.float32))
    x = f(inputs["x"])
    c = f(inputs["c"])
    w_in = f(inputs["w_in"])[0].reshape(8, 128, 8192)
    Wp = f(w_in[:, :, :4096].reshape(8, 128, 4, 8, 128).transpose(3, 1, 0, 2, 4))
    Wpl = f(w_in[:, :, 4096:6144].reshape(8, 128, 2, 4, 2, 128).transpose(3, 1, 0, 2, 4, 5).reshape(4, 128, 8, 4, 128))
    Wm = f(w_in[:, :, 6144:8192].reshape(8, 128, 2, 8, 128).transpose(3, 1, 0, 2, 4))
    wa = f(inputs["w_branch_a"])[0].reshape(8, 128, 8, 128)
    wb = f(inputs["w_branch_b"])[0].reshape(8, 128, 8, 128)
    Wab = f(np.stack([wa, wb], axis=3).transpose(2, 1, 0, 3, 4))
    w_out_l = f(f(inputs["w_out"])[0].reshape(8, 128, 1024).transpose(1, 0, 2))
    w_ada_l = f(f(inputs["w_ada"])[0].reshape(8, 128, 3072).transpose(1, 0, 2))
    pool_w_l = f(f(inputs["pool_w"])[0].reshape(4, 2, 128, 256).transpose(2, 0, 1, 3))
    pool_scale_l = f(f(inputs["pool_scale"])[0].reshape(8, 128).T)
    cb, cf = _consts()
    shared = {
        "w_ada_l": w_ada_l, "b_ada": f(inputs["b_ada"]).reshape(1, 3072),
        "norm_gain": f(inputs["norm_gain"]).reshape(1, D), "final_gain": f(inputs["final_gain"]).reshape(1, D),
        "Wp": Wp, "Wpl": Wpl, "Wm": Wm, "Wab": Wab, "w_out_l": w_out_l, "pool_w_l": pool_w_l,
        "pool_scale_l": pool_scale_l, "constsB": cb, "constsF": cf,
    }
    maps = []
    for i in range(ncores):
        xs = x[i * nseq:(i + 1) * nseq].reshape(nseq * SEQ, D)
        cs = c[i * nseq:(i + 1) * nseq]
        cT = f(cs.reshape(nseq, 8, 128).transpose(2, 1, 0))
        m = dict(shared)
        m["x"] = f(xs)
        m["cT"] = cT
        maps.append(m)
    return maps


def kernel(**inputs):
    nc = build(NSEQ_CORE)
    in_maps = make_in_maps(inputs)
    res = run_bass_kernel_spmd(nc, in_maps, core_ids=list(range(NCORES)))
    outs = [np.asarray(r["out"], dtype=np.float32).reshape(NSEQ_CORE, SEQ, D) for r in res.results]
    return np.concatenate(outs, axis=0)
```

```python
import numpy as np
from contextlib import ExitStack
import concourse.bass as bass
import concourse.mybir as mybir
from concourse.bass_utils import run_bass_kernel_spmd

F32 = mybir.dt.float32
BF16 = mybir.dt.bfloat16
AF = mybir.ActivationFunctionType
ALU = mybir.AluOpType

NCORES = 8
SEQ = 2048
D = 1024
NSEQ_CORE = 4
EPS = 1e-6
NCH = 4
NTT = 16
POOL_WINDOWS = (2, 4, 8, 16)
USE_SOFTPLUS = True

CB_ID, CB_MTRI, CB_TRI, CB_ES, CB_L = 0, 128, 256, 384, 384 + 16 * 192
CB_N = CB_L + 2048
CF_SGN, CF_MLO, CF_CNT = 0, 1, 2
CF_NH = 2 + 64
CF_EPS = 2 + 64 + 1
CF_N = 2 + 64 + 2


class Buf:
    __slots__ = ("w", "r", "dsem", "dcnt", "uid")
    _n = 0

    def __init__(self):
        self.w = None
        self.r = {}
        self.dsem = None
        self.dcnt = 0
        Buf._n += 1
        self.uid = Buf._n


class KB:
    def __init__(self, nc, es):
        self.nc = nc
        self.es = es
        self.eng = {"pe": nc.tensor, "act": nc.scalar, "dve": nc.vector, "pool": nc.gpsimd, "sp": nc.sync}
        self.sem = {k: es.enter_context(nc.semaphore("s_" + k)) for k in self.eng}
        self.cnt = {k: 0 for k in self.eng}
        self.waited = {}
        self.dbufs = []
        self.free_sems = {True: [], False: []}
        self.nsem = 0
        self.nwaits = 0

    def _wait(self, e, tok):
        if tok is None:
            return
        key, val = tok
        if isinstance(key, str):
            if key == e and e == "pe":
                return
            k = (e, key)
            sem = self.sem[key]
        else:
            if key.dsem is None or key.dcnt < val:
                return
            k = (e, key.uid)
            sem = key.dsem
        if self.waited.get(k, 0) >= val:
            return
        self.eng[e].wait_ge(sem, val)
        self.nwaits += 1
        self.waited[k] = val

    def _deps(self, e, reads, writes):
        for b in reads:
            self._wait(e, b.w)
        for b in writes:
            self._wait(e, b.w)
            for t in b.r.values():
                self._wait(e, t)

    def op(self, e, fn, reads=(), writes=(), sig=True):
        self._deps(e, reads, writes)
        inst = fn(self.eng[e])
        if sig:
            self.cnt[e] += 1
            inst.then_inc(self.sem[e], 1)
            tok = (e, self.cnt[e])
        else:
            tok = (e, self.cnt[e] + 1)
        for b in reads:
            b.r[e] = tok
        for b in writes:
            b.w = tok
            b.r = {}
        return inst

    def dma(self, q, out, in_, sb, reads=(), writes=()):
        self._deps(q, reads, writes)
        if sb.dsem is None:
            pool_ = self.free_sems[q == "pool"]
            if pool_:
                sb.dsem, sb.dcnt = pool_.pop()
            else:
                self.nsem += 1
                sb.dsem = self.es.enter_context(self.nc.semaphore("d%d" % self.nsem))
                sb.dcnt = 0
            self.dbufs.append((sb, q == "pool"))
        inst = self.eng[q].dma_start(out=out, in_=in_)
        sb.dcnt += 16
        inst.then_inc(sb.dsem, 16)
        tok = (sb, sb.dcnt)
        for b in reads:
            b.r[("dma", sb.uid)] = tok
        for b in writes:
            b.w = tok
            b.r = {}

    def barrier(self):
        for e in self.eng:
            for f in ("pe", "act", "dve", "pool"):
                if f != e and self.cnt[f] > 0:
                    self._wait(e, (f, self.cnt[f]))
            for b, _sw in self.dbufs:
                self._wait(e, (b, b.dcnt))
        for b, sw in self.dbufs:
            self.free_sems[sw].append((b.dsem, b.dcnt))
            b.dsem = None
            b.dcnt = -1
        self.dbufs = []


def build(nseq=NSEQ_CORE):
    nc = bass.Bass("TRN2", target_bir_lowering=False)
    NTOK = nseq * SEQ

    def din(name, shape):
        return nc.dram_tensor(name, list(shape), F32, kind="ExternalInput").ap()

    x_d = din("x", [NTOK, D])
    cT_d = din("cT", [128, 8, nseq])
    wada_d = din("w_ada_l", [128, 8, 3072])
    bada_d = din("b_ada", [1, 3072])
    ngain_d = din("norm_gain", [1, D])
    fgain_d = din("final_gain", [1, D])
    wp_d = din("Wp", [8, 128, 8, 4, 128])
    wpl_d = din("Wpl", [4, 128, 8, 4, 128])
    wm_d = din("Wm", [8, 128, 8, 2, 128])
    wab_d = din("Wab", [8, 128, 8, 2, 128])
    wout_d = din("w_out_l", [128, 8, 1024])
    poolw_d = din("pool_w_l", [128, 4, 2, 256])
    pscale_d = din("pool_scale_l", [128, 8])
    cb_d = din("constsB", [128, CB_N])
    cf_d = din("constsF", [128, CF_N])
    out_d = nc.dram_tensor("out", [NTOK, D], F32, kind="ExternalOutput").ap()
    mod_d = nc.dram_tensor("mod_scratch", [nseq, 3072], F32, kind="Internal").ap()

    with ExitStack() as es:
        kb = KB(nc, es)
        uid = [0]

        def sb(name, shape, dt, stack):
            uid[0] += 1
            return stack.enter_context(nc.sbuf_tensor("%s_%d" % (name, uid[0]), list(shape), dt))

        def psum_banks(stack, n, nbf=0):
            uid[0] += 1
            ps = [stack.enter_context(nc.psum_tensor("ps%d_%d" % (i, uid[0]), [128, 512], F32)) for i in range(n)]
            pb = [stack.enter_context(nc.psum_tensor("pb%d_%d" % (i, uid[0]), [128, 8, 128], BF16)) for i in range(nbf)]
            return ps, [Buf() for _ in range(n)], pb, [Buf() for _ in range(nbf)]

        cB = sb("cB", [128, CB_N], BF16, es)
        cF = sb("cF", [128, CF_N], F32, es)
        pscale = sb("pscale", [128, 8], F32, es)
        hT = sb("hT", [128, 8, SEQ], BF16, es)

        b_cB, b_cF = Buf(), Buf()
        b_const = Buf()
        b_hT = [Buf() for _ in range(NCH)]
        b_mod = Buf()

        ident = cB[:, CB_ID:CB_ID + 128]
        mtri = cB[:, CB_MTRI:CB_MTRI + 128]
        trineg = cB[:, CB_TRI:CB_TRI + 128]

        def es_sel(a, odd):
            o = CB_ES + a * 192 + (64 if odd else 0)
            return cB[:, o:o + 128]

        kb.dma("pool", cB[:], cb_d[:, :], b_cB, writes=[b_cB])
        kb.dma("sp", cF[:], cf_d[:, :], b_cF, writes=[b_cF])
        kb.dma("sp", pscale[:], pscale_d[:, :], b_cF, writes=[b_cF])

        with ExitStack() as ph:
            PS, b_PS, _, _ = psum_banks(ph, 2)
            cTt = sb("cTt", [128, 8, nseq], F32, ph)
            scT = sb("scT", [128, 8, nseq], BF16, ph)
            wad = [sb("wad%d" % i, [128, 8, 1024], BF16, ph) for i in range(3)]
            bad = sb("bad", [nseq, 3072], F32, ph)
            modS = sb("modS", [nseq, 3072], F32, ph)
            b_c, b_sc, b_bad, b_modS = Buf(), Buf(), Buf(), Buf()
            b_wad = [Buf() for _ in range(3)]
            kb.dma("sp", cTt[:], cT_d[:, :, :], b_c, writes=[b_c])
            kb.dma("sp", bad[:], bada_d[0:1, :].partition_broadcast(nseq), b_bad, writes=[b_bad])
            for kind in range(3):
                kb.dma("pool", wad[kind][:], wada_d[:, :, kind * 1024:(kind + 1) * 1024], b_wad[kind], writes=[b_wad[kind]])
            kb.op("act", lambda e: e.activation(out=scT[:], in_=cTt[:], func=AF.Silu), reads=[b_c], writes=[b_sc])
            for kind in range(3):
                for half in range(2):
                    pb = (kind * 2 + half) % 2
                    pst = PS[pb]
                    for kt in range(8):
                        kb.op("pe", lambda e: e.matmul(pst[0:nseq, :], scT[:, kt, :], wad[kind][:, kt, half * 512:(half + 1) * 512],
                                                       start=(kt == 0), stop=(kt == 7)),
                              reads=[b_sc, b_wad[kind]], writes=[b_PS[pb]], sig=(kt == 7))
                    col = kind * 1024 + half * 512
                    kb.op("dve", lambda e: e.tensor_tensor(out=modS[:, col:col + 512], in0=pst[0:nseq, :], in1=bad[:, col:col + 512], op=ALU.add),
                          reads=[b_PS[pb], b_bad], writes=[b_modS])
            kb.dma("sp", mod_d[:, :], modS[:], b_modS, reads=[b_modS], writes=[b_mod])
            kb.barrier()

        for bl in range(nseq):
            tok0 = bl * SEQ
            with ExitStack() as ph:
                _, _, PB, b_PB = psum_banks(ph, 0, 2)
                G = sb("G", [128, D], F32, ph)
                gain_bc = sb("gain_bc", [128, D], F32, ph)
                shift = sb("shift", [128, D], F32, ph)
                sctmp = sb("sctmp", [128, D], F32, ph)
                xt = [sb("xt%d" % i, [128, D], F32, ph) for i in range(4)]
                junk = sb("junk", [128, D], BF16, ph)
                t1 = [sb("t1%d" % i, [128, D], F32, ph) for i in range(2)]
                hb = [sb("hb%d" % i, [128, D], BF16, ph) for i in range(2)]
                ss = sb("ss", [128, NTT], F32, ph)
                rstd = sb("rstd", [128, NTT], F32, ph)
                b_G, b_shift, b_sct, b_gain, b_junk = Buf(), Buf(), Buf(), Buf(), Buf()
                b_xt = [Buf() for _ in range(4)]
                b_t1 = [Buf(), Buf()]
                b_hb = [Buf(), Buf()]
                b_ss = [Buf() for _ in range(NTT)]
                b_rstd = [Buf() for _ in range(NTT)]
                kb.dma("sp", gain_bc[:], ngain_d[0:1, :].partition_broadcast(128), b_gain, writes=[b_gain])
                kb.dma("sp", shift[:], mod_d[bl:bl + 1, 0:1024].partition_broadcast(128), b_shift, reads=[b_mod], writes=[b_shift])
                kb.dma("sp", sctmp[:], mod_d[bl:bl + 1, 1024:2048].partition_broadcast(128), b_sct, reads=[b_mod], writes=[b_sct])
                kb.op("dve", lambda e: e.scalar_tensor_tensor(out=G[:], in0=sctmp[:], scalar=1.0, in1=gain_bc[:], op0=ALU.add, op1=ALU.mult),
                      reads=[b_sct, b_gain], writes=[b_G])
                def x_stage1(tt):
                    s = tt % 4
                    kb.dma("sp", xt[s][:], x_d[tok0 + tt * 128: tok0 + (tt + 1) * 128, :], b_xt[s], writes=[b_xt[s]])
                    kb.op("act", lambda e: e.activation(out=junk[:], in_=xt[s][:], func=AF.Square, accum_out=ss[:, tt:tt + 1]),
                          reads=[b_xt[s]], writes=[b_junk, b_ss[tt]])
                    kb.op("act", lambda e: e.activation(out=rstd[:, tt:tt + 1], in_=ss[:, tt:tt + 1], func=AF.Ln, bias=cF[:, CF_EPS:CF_EPS + 1], scale=1.0 / D),
                          reads=[b_ss[tt]], writes=[b_rstd[tt]])
                    kb.op("act", lambda e: e.activation(out=rstd[:, tt:tt + 1], in_=rstd[:, tt:tt + 1], func=AF.Exp, scale=-0.5),
                          reads=[b_rstd[tt]], writes=[b_rstd[tt]])

                def x_stage2(tt):
                    s = tt % 2
                    s4 = tt % 4
                    kb.op("dve", lambda e: e.scalar_tensor_tensor(out=t1[s][:], in0=xt[s4][:], scalar=rstd[:, tt:tt + 1], in1=G[:],
                                                                  op0=ALU.mult, op1=ALU.mult),
                          reads=[b_xt[s4], b_rstd[tt], b_G], writes=[b_t1[s]])
                    kb.op("pool", lambda e: e.tensor_tensor(out=hb[s][:], in0=t1[s][:], in1=shift[:], op=ALU.add),
                          reads=[b_t1[s], b_shift], writes=[b_hb[s]])

                def x_stage3(tt):
                    s = tt % 2
                    for kt in range(8):
                        kb.op("pe", lambda e: e.transpose(PB[s][:, kt, :], hb[s][:, kt * 128:(kt + 1) * 128], ident),
                              reads=[b_hb[s]], writes=[b_PB[s]], sig=(kt == 7))
                    kb.op("act", lambda e: e.activation(out=hT[:, :, tt * 128:(tt + 1) * 128], in_=PB[s][:, :, :], func=AF.Copy),
                          reads=[b_PB[s]], writes=[b_hT[tt // 4]])

                for i in range(NTT + 2):
                    if i < NTT:
                        x_stage1(i)
                    if 0 <= i - 1 < NTT:
                        x_stage2(i - 1)
                    if 0 <= i - 2 < NTT:
                        x_stage3(i - 2)
                kb.barrier()

            with ExitStack() as phga:
                ga = sb("ga", [128, 8, SEQ], BF16, phga)
                b_ga = [[Buf() for _ in range(NCH)] for _ in range(8)]
                with ExitStack() as ph:
                    PS, b_PS, _, _ = psum_banks(ph, 8)
                    PB = PS[7][:].bitcast(BF16).rearrange("p (k c) -> p k c", k=8)
                    b_PB = b_PS[7]
                    NSL = 2
                    wp = [sb("wp%d" % i, [128, 8, 4, 128], BF16, ph) for i in range(NSL)]
                    qa = [[sb("qa%d_%d" % (i, h), [128, SEQ], BF16, ph) for h in range(2)] for i in range(NSL)]
                    ka = [[sb("ka%d_%d" % (i, h), [128, SEQ], BF16, ph) for h in range(2)] for i in range(NSL)]
                    zg = [sb("zg%d" % i, [128, SEQ], F32, ph) for i in range(NSL)]
                    vT = sb("vT", [128, SEQ], BF16, ph)
                    vt = [sb("vt%d" % i, [128, NTT, 128], BF16, ph) for i in range(NSL)]
                    sp = [sb("sp%d" % h, [128, 16, 512], BF16, ph) for h in range(2)]
                    NW = 4
                    wt = [sb("wt%d" % i, [128, 512], BF16, ph) for i in range(NW)]
                    HS = sb("HS", [128, 512], BF16, ph)
                    etmp = [] if USE_SOFTPLUS else [sb("etmp%d" % i, [128, 512], F32, ph) for i in range(3)]
                    b_etmp = [Buf() for _ in range(3)]
                    b_wp = [Buf() for _ in range(NSL)]
                    b_qa = [[[Buf() for _ in range(NCH)] for _ in range(2)] for _ in range(NSL)]
                    b_ka = [[[Buf() for _ in range(NCH)] for _ in range(2)] for _ in range(NSL)]
                    b_zg = [[Buf() for _ in range(NCH)] for _ in range(NSL)]
                    b_vT = [Buf() for _ in range(NCH)]
                    b_vt = [[Buf(), Buf()] for _ in range(NSL)]
                    b_sp = [[Buf() for _ in range(16)] for _ in range(2)]
                    b_wt = [Buf() for _ in range(NW)]
                    b_HS = Buf()
                    kb.dma("pool", wp[0][:], wp_d[0], b_wp[0], writes=[b_wp[0]])
                    kb.dma("pool", wp[1][:], wp_d[1], b_wp[1], writes=[b_wp[1]])
                    for i in range(NSL):
                        kb.dma("pool", ka[i][0][64:128, :], cb_d[64:128, CB_L:CB_L + 2048], Buf(), writes=[b_ka[i][0][c] for c in range(NCH)])
                        kb.dma("pool", ka[i][1][0:64, :], cb_d[0:64, CB_L:CB_L + 2048], Buf(), writes=[b_ka[i][1][c] for c in range(NCH)])
                    state = {"ww": 0, "pp": 0}
                    XB = [1, 2, 3]
                    PROJB = [0, 7]

                    def proj_pair(j, overlapped):
                        sl = j % NSL
                        kb.op("pool", lambda e: e.memset(qa[sl][0][64:128, :], 0.0), writes=[b_qa[sl][0][c] for c in range(NCH)])
                        kb.op("pool", lambda e: e.memset(qa[sl][1][0:64, :], 0.0), writes=[b_qa[sl][1][c] for c in range(NCH)])
                        for g in range(4):
                            for c in range(NCH):
                                pi = PROJB[state["pp"] % 2]
                                state["pp"] += 1
                                pst = PS[pi]
                                for kt in range(8):
                                    kb.op("pe", lambda e: e.matmul(pst[:], wp[sl][:, kt, g, :], hT[:, kt, c * 512:(c + 1) * 512],
                                                                   start=(kt == 0), stop=(kt == 7)),
                                          reads=[b_wp[sl], b_hT[c]], writes=[b_PS[pi]], sig=(kt == 7))
                                    if overlapped and kt == 3:
                                        yield
                                cs = slice(c * 512, (c + 1) * 512)
                                if g == 0 or g == 1:
                                    T, bT = (qa, b_qa) if g == 0 else (ka, b_ka)
                                    kb.op("dve", lambda e: e.tensor_copy(out=T[sl][0][0:64, cs], in_=pst[0:64, :]),
                                          reads=[b_PS[pi]], writes=[bT[sl][0][c]])
                                    kb.op("dve", lambda e: e.tensor_copy(out=T[sl][1][64:128, cs], in_=pst[64:128, :]),
                                          reads=[b_PS[pi]], writes=[bT[sl][1][c]])
                                elif g == 2:
                                    kb.op("dve", lambda e: e.tensor_copy(out=vT[:, cs], in_=pst[:]),
                                          reads=[b_PS[pi]], writes=[b_vT[c]])
                                else:
                                    kb.op("dve", lambda e: e.tensor_copy(out=zg[sl][:, cs], in_=pst[:]),
                                          reads=[b_PS[pi]], writes=[b_zg[sl][c]])
                                yield
                        for hf in range(2):
                            for k8 in range(8):
                                kbi = hf * 8 + k8
                                kb.op("pe", lambda e: e.transpose(PB[:, k8, :], vT[:, kbi * 128:(kbi + 1) * 128], ident),
                                      reads=[b_vT[kbi // 4]], writes=[b_PB], sig=(k8 == 7))
                            kb.op("dve", lambda e: e.tensor_copy(out=vt[sl][:, hf * 8:(hf + 1) * 8, :], in_=PB[:, :, :]),
                                  reads=[b_PB], writes=[b_vt[sl][hf]])
                            yield

                    def silu_zg(sl):
                        kb.op("act", lambda e: e.activation(out=zg[sl][:, :], in_=zg[sl][:, :], func=AF.Silu),
                              reads=[b_zg[sl][c] for c in range(NCH)], writes=[b_zg[sl][c] for c in range(NCH)])

                    def attn_pair(j):
                        sl = j % NSL
                        SK = 2
                        for c in range(NCH):
                            n = 4 * c + 4
                            cbase = c * 512
                            blocks = [(hd, a) for a in range(n) for hd in range(2)]
                            NB = len(blocks)

                            def lo_of(a):
                                return max(0, a - 4 * c) * 128

                            def p1_front(i):
                                hd, a = blocks[i]
                                lo = lo_of(a)
                                xi = XB[i % len(XB)]
                                X = PS[xi]
                                kb.op("pe", lambda e: e.matmul(X[:, lo:512], ka[sl][hd][:, a * 128:(a + 1) * 128],
                                                               qa[sl][hd][:, cbase + lo:cbase + 512], start=True, stop=True),
                                      reads=[b_ka[sl][hd][a // 4], b_qa[sl][hd][c]], writes=[b_PS[xi]])
                                if USE_SOFTPLUS:
                                    kb.op("act", lambda e: e.activation(out=sp[hd][:, a, lo:512], in_=X[:, lo:512], func=AF.Softplus, scale=0.125),
                                          reads=[b_PS[xi]], writes=[b_sp[hd][a]])
                                else:
                                    ei = i % 3
                                    kb.op("act", lambda e: e.activation(out=etmp[ei][:, lo:512], in_=X[:, lo:512], func=AF.Exp, scale=0.125),
                                          reads=[b_PS[xi]], writes=[b_etmp[ei]])
                                    kb.op("act", lambda e: e.activation(out=sp[hd][:, a, lo:512], in_=etmp[ei][:, lo:512], func=AF.Ln, bias=1.0, scale=1.0),
                                          reads=[b_etmp[ei]], writes=[b_sp[hd][a]])
                                if a >= 4 * c:
                                    kb.op("dve", lambda e: e.tensor_tensor(out=sp[hd][:, a, lo:lo + 128], in0=sp[hd][:, a, lo:lo + 128],
                                                                           in1=mtri, op=ALU.mult),
                                          reads=[b_sp[hd][a]], writes=[b_sp[hd][a]])

                            def p1_back(i):
                                hd, a = blocks[i]
                                lo = lo_of(a)
                                kb.op("pe", lambda e: e.matmul(PS[4][:, lo:512], es_sel(a, hd == 1), sp[hd][:, a, lo:512],
                                                               start=(i == 0), stop=(i == NB - 1)),
                                      reads=[b_sp[hd][a]], writes=[b_PS[4]], sig=(i == NB - 1))

                            for i in range(NB + SK):
                                if i < NB:
                                    p1_front(i)
                                if i - SK >= 0:
                                    p1_back(i - SK)
                                yield "step1"
                            kb.op("dve", lambda e: e.tensor_scalar(out=HS[:], in0=PS[4][:], scalar1=cF[:, CF_SGN:CF_SGN + 1], scalar2=None,
                                                                   op0=ALU.mult), reads=[b_PS[4]], writes=[b_HS])
                            kb.op("dve", lambda e: e.scalar_tensor_tensor(out=qa[sl][0][64:128, cbase:cbase + 512], in0=PS[4][64:128, :],
                                                                          scalar=cF[64:128, CF_MLO:CF_MLO + 1], in1=HS[64:128, :],
                                                                          op0=ALU.mult, op1=ALU.add),
                                  reads=[b_PS[4], b_HS], writes=[b_qa[sl][0][c]])
                            kb.op("dve", lambda e: e.scalar_tensor_tensor(out=qa[sl][1][0:64, cbase:cbase + 512], in0=PS[4][0:64, :],
                                                                          scalar=cF[0:64, CF_MLO:CF_MLO + 1], in1=HS[0:64, :],
                                                                          op0=ALU.mult, op1=ALU.add),
                                  reads=[b_PS[4], b_HS], writes=[b_qa[sl][1][c]])
                            if c == 0:
                                silu_zg(sl)
                            yield "drain"
                            wslot = {}

                            def p3_front(i):
                                hd, a = blocks[i]
                                lo = lo_of(a)
                                xi = XB[i % len(XB)]
                                X = PS[xi]
                                kb.op("pe", lambda e: e.matmul(X[:, lo:512], ka[sl][hd][:, a * 128:(a + 1) * 128],
                                                               qa[sl][hd][:, cbase + lo:cbase + 512], start=True, stop=False),
                                      reads=[b_ka[sl][hd][a // 4], b_qa[sl][hd][c]], writes=[b_PS[xi]], sig=False)
                                kb.op("pe", lambda e: e.matmul(X[:, lo:512], trineg, sp[hd][:, a, lo:512], start=False, stop=True),
                                      reads=[b_sp[hd][a]], writes=[b_PS[xi]])
                                wi = state["ww"] % NW
                                state["ww"] += 1
                                wslot[i] = wi
                                kb.op("act", lambda e: e.activation(out=wt[wi][:, lo:512], in_=X[:, lo:512], func=AF.Exp, scale=0.125),
                                      reads=[b_PS[xi]], writes=[b_wt[wi]])
                                if a >= 4 * c:
                                    kb.op("dve", lambda e: e.tensor_tensor(out=wt[wi][:, lo:lo + 128], in0=wt[wi][:, lo:lo + 128],
                                                                           in1=mtri, op=ALU.mult),
                                          reads=[b_wt[wi]], writes=[b_wt[wi]])

                            def p3_back(i):
                                hd, a = blocks[i]
                                lo = lo_of(a)
                                wi = wslot[i]
                                oi = 5 + hd
                                rows = slice(0, 64) if hd == 0 else slice(64, 128)
                                kb.op("pe", lambda e: e.matmul(PS[oi][:, lo:512], vt[sl][:, a, :], wt[wi][:, lo:512],
                                                               start=(a == 0), stop=(a == n - 1)),
                                      reads=[b_vt[sl][a // 8], b_wt[wi]], writes=[b_PS[oi]], sig=(a == n - 1))
                                if a == n - 1:
                                    kb.op("dve", lambda e: e.tensor_tensor(out=ga[rows, j, cbase:cbase + 512], in0=PS[oi][rows, :],
                                                                           in1=zg[sl][rows, cbase:cbase + 512], op=ALU.mult),
                                          reads=[b_PS[oi], b_zg[sl][c]], writes=[b_ga[j][c]])

                            for i in range(NB + SK):
                                if i < NB:
                                    p3_front(i)
                                if i - SK >= 0:
                                    p3_back(i - SK)
                                yield "step"
                            yield "drain"

                    def run_interleaved(gmain, gside, ratio, ndrain):
                        k = 0
                        side_done = gside is None

                        def side(nsteps):
                            nonlocal side_done
                            for _ in range(nsteps):
                                if side_done:
                                    return
                                try:
                                    next(gside)
                                except StopIteration:
                                    side_done = True

                        for tag in gmain:
                            if tag == "drain":
                                side(ndrain)
                            elif tag == "step1":
                                k += 1
                                if k % ratio == 0:
                                    side(1)
                        if not side_done:
                            for _ in gside:
                                pass

                    for _ in proj_pair(0, False):
                        pass
                    for j in range(8):
                        if j + 2 < 8:
                            kb.dma("pool", wp[j % NSL][:], wp_d[j + 2], b_wp[j % NSL], writes=[b_wp[j % NSL]])
                        nxt = proj_pair(j + 1, True) if j + 1 < 8 else None
                        run_interleaved(attn_pair(j), nxt, 8, 3)
                    kb.barrier()

                with ExitStack() as phgb:
                    gb = sb("gb", [128, 8, SEQ], BF16, phgb)
                    b_gb = [[Buf() for _ in range(NCH)] for _ in range(8)]
                    with ExitStack() as ph:
                        PS, b_PS, _, _ = psum_banks(ph, 6)
                        wg = [sb("wg%d" % i, [128, 8, 4, 128], BF16, ph) for i in range(2)]
                        poolw = sb("poolw", [128, 4, 2, 256], BF16, ph)
                        U = [[sb("U%d_%d" % (s_, ci), [128, 16 + 512], F32, ph) for ci in range(2)] for s_ in range(2)]
                        SA = [sb("SA%d" % ci, [128, 16 + 512], F32, ph) for ci in range(2)]
                        SB_ = [sb("SB%d" % ci, [128, 16 + 512], F32, ph) for ci in range(2)]
                        dT = [[sb("dT%d_%d" % (s_, ci), [128, 512], BF16, ph) for ci in range(2)] for s_ in range(2)]
                        zbs = [[sb("zbs%d_%d" % (s_, dt_), [128, 512], F32, ph) for dt_ in range(2)] for s_ in range(2)]
                        fx = sb("fx", [128, 16], F32, ph)
                        b_wg = [Buf(), Buf()]
                        b_poolw, b_fx = Buf(), Buf()
                        b_U = [[Buf(), Buf()], [Buf(), Buf()]]
                        b_SA = [Buf(), Buf()]
                        b_SB = [Buf(), Buf()]
                        b_dT = [[Buf(), Buf()], [Buf(), Buf()]]
                        b_zbs = [[Buf(), Buf()], [Buf(), Buf()]]
                        kb.dma("pool", poolw[:], poolw_d[:, :, :, :], b_poolw, writes=[b_poolw])
                        state_p = {"pp": 0}

                        def p_stage_a(g, c, s_):
                            win = POOL_WINDOWS[g]
                            gs = g % 2
                            cs = slice(c * 512, (c + 1) * 512)
                            if c == 0:
                                if g == 0:
                                    kb.dma("pool", wg[0][:], wpl_d[0], b_wg[0], writes=[b_wg[0]])
                                if g + 1 < 4:
                                    kb.dma("pool", wg[(g + 1) % 2][:], wpl_d[g + 1], b_wg[(g + 1) % 2], writes=[b_wg[(g + 1) % 2]])
                                for ci in range(2):
                                    kb.op("pool", lambda e: e.memset(U[s_][ci][:, 0:16], 0.0), writes=[b_U[s_][ci]])
                            for which in range(4):
                                pi = state_p["pp"] % 4
                                state_p["pp"] += 1
                                pst = PS[pi]
                                for kt in range(8):
                                    kb.op("pe", lambda e: e.matmul(pst[:], wg[gs][:, kt, which, :], hT[:, kt, cs],
                                                                   start=(kt == 0), stop=(kt == 7)),
                                          reads=[b_wg[gs], b_hT[c]], writes=[b_PS[pi]], sig=(kt == 7))
                                if which < 2:
                                    ci = which
                                    kb.op("act", lambda e: e.activation(out=U[s_][ci][:, 16:528], in_=pst[:], func=AF.Copy),
                                          reads=[b_PS[pi]], writes=[b_U[s_][ci]])
                                else:
                                    dt_ = which - 2
                                    kb.op("act", lambda e: e.activation(out=zbs[s_][dt_][:], in_=pst[:], func=AF.Silu),
                                          reads=[b_PS[pi]], writes=[b_zbs[s_][dt_]])
                            for ci in range(2):
                                eng = "pool" if ci == 0 else "dve"
                                u = U[s_][ci]
                                bu = b_U[s_][ci]
                                if c + 1 < NCH:
                                    kb.op("pool", lambda e: e.tensor_copy(out=U[1 - s_][ci][:, 0:16], in_=u[:, 512:528]),
                                          reads=[bu], writes=[b_U[1 - s_][ci]])
                                cur, bcur = u, bu
                                nxts = [(SA[ci], b_SA[ci]), (SB_[ci], b_SB[ci])]
                                first = 0
                                for step in range(g + 1):
                                    sh = 1 << step
                                    nx, bnx = nxts[step % 2]
                                    f0 = first + sh
                                    kb.op(eng, lambda e: e.tensor_tensor(out=nx[:, f0:528], in0=cur[:, f0:528],
                                                                         in1=cur[:, f0 - sh:528 - sh], op=ALU.add),
                                          reads=[bcur], writes=[bnx])
                                    cur, bcur = nx, bnx
                                    first = f0
                                kb.op("dve", lambda e: e.scalar_tensor_tensor(out=dT[s_][ci][:], in0=cur[:, 16:528], scalar=1.0 / win,
                                                                              in1=u[:, 16:528], op0=ALU.mult, op1=ALU.subtract),
                                      reads=[bcur, bu], writes=[b_dT[s_][ci]])
                                if c == 0:
                                    kb.op("dve", lambda e: e.tensor_tensor(out=fx[:], in0=cur[:, 16:32],
                                                                           in1=cF[:, CF_CNT + g * 16:CF_CNT + (g + 1) * 16], op=ALU.mult),
                                          reads=[bcur], writes=[b_fx])
                                    kb.op("dve", lambda e: e.tensor_tensor(out=dT[s_][ci][:, 0:16], in0=fx[:], in1=u[:, 16:32], op=ALU.subtract),
                                          reads=[b_fx, bu, b_dT[s_][ci]], writes=[b_dT[s_][ci]])

                        def p_stage_b(g, c, s_):
                            cs = slice(c * 512, (c + 1) * 512)
                            for dt_ in range(2):
                                ft = 2 * g + dt_
                                pi = 4 + dt_
                                pst = PS[pi]
                                for ci in range(2):
                                    kb.op("pe", lambda e: e.matmul(pst[:], poolw[:, g, ci, dt_ * 128:(dt_ + 1) * 128], dT[s_][ci][:],
                                                                   start=(ci == 0), stop=(ci == 1)),
                                          reads=[b_poolw, b_dT[s_][ci]], writes=[b_PS[pi]], sig=(ci == 1))
                                kb.op("dve", lambda e: e.scalar_tensor_tensor(out=gb[:, ft, cs], in0=pst[:],
                                                                              scalar=pscale[:, ft:ft + 1], in1=zbs[s_][dt_][:],
                                                                              op0=ALU.mult, op1=ALU.mult),
                                      reads=[b_PS[pi], b_zbs[s_][dt_]], writes=[b_gb[ft][c]])

                        steps = [(g, c) for g in range(4) for c in range(NCH)]
                        for k in range(len(steps) + 1):
                            if k < len(steps):
                                p_stage_a(steps[k][0], steps[k][1], k % 2)
                            if k - 1 >= 0:
                                p_stage_b(steps[k - 1][0], steps[k - 1][1], (k - 1) % 2)
                        kb.barrier()

                    with ExitStack() as phf:
                        mg = sb("mg", [128, 8, SEQ], BF16, phf)
                        wout = sb("wout", [128, 8, 1024], BF16, phf)
                        b_mg = [[Buf() for _ in range(NCH)] for _ in range(8)]
                        b_wout = Buf()
                        kb.dma("pool", wout[:], wout_d[:, :, :], b_wout, writes=[b_wout])
                        with ExitStack() as ph:
                            PS, b_PS, _, _ = psum_banks(ph, 8)
                            NS3 = 3
                            wab = [sb("wab%d" % i, [128, 8, 2, 128], BF16, ph) for i in range(NS3)]
                            wm = [sb("wm%d" % i, [128, 8, 2, 128], BF16, ph) for i in range(NS3)]
                            sgA = [sb("sgA%d" % i, [128, 512], F32, ph) for i in range(2)]
                            sgB = [sb("sgB%d" % i, [128, 512], F32, ph) for i in range(2)]
                            tA = [sb("tA%d" % i, [128, 512], F32, ph) for i in range(2)]
                            tB = [sb("tB%d" % i, [128, 512], F32, ph) for i in range(2)]
                            b_wab = [Buf() for _ in range(NS3)]
                            b_wm = [Buf() for _ in range(NS3)]
                            b_sgA, b_sgB, b_tA, b_tB = [Buf(), Buf()], [Buf(), Buf()], [Buf(), Buf()], [Buf(), Buf()]

                            def load_m(m):
                                s3 = m % NS3
                                kb.dma("pool", wab[s3][:], wab_d[m], b_wab[s3], writes=[b_wab[s3]])
                                kb.dma("pool", wm[s3][:], wm_d[m], b_wm[s3], writes=[b_wm[s3]])

                            load_m(0)
                            load_m(1)
                            step = 0
                            for m in range(8):
                                s3 = m % NS3
                                if m + 2 < 8:
                                    load_m(m + 2)
                                for c in range(NCH):
                                    cs = slice(c * 512, (c + 1) * 512)
                                    st2 = step % 2
                                    step += 1
                                    pA, pB_, pMa, pMb = [4 * st2 + k_ for k_ in range(4)]
                                    for kt in range(8):
                                        kb.op("pe", lambda e: e.matmul(PS[pMa][:], wm[s3][:, kt, 0, :], hT[:, kt, cs], start=(kt == 0), stop=(kt == 7)),
                                              reads=[b_wm[s3], b_hT[c]], writes=[b_PS[pMa]], sig=(kt == 7))
                                    for kt in range(8):
                                        kb.op("pe", lambda e: e.matmul(PS[pMb][:], wm[s3][:, kt, 1, :], hT[:, kt, cs], start=(kt == 0), stop=(kt == 7)),
                                              reads=[b_wm[s3], b_hT[c]], writes=[b_PS[pMb]], sig=(kt == 7))
                                    for kt in range(8):
                                        kb.op("pe", lambda e: e.matmul(PS[pA][:], wab[s3][:, kt, 0, :], ga[:, kt, cs], start=(kt == 0), stop=(kt == 7)),
                                              reads=[b_wab[s3], b_ga[kt][c]], writes=[b_PS[pA]], sig=(kt == 7))
                                    for kt in range(8):
                                        kb.op("pe", lambda e: e.matmul(PS[pB_][:], wab[s3][:, kt, 1, :], gb[:, kt, cs], start=(kt == 0), stop=(kt == 7)),
                                              reads=[b_wab[s3], b_gb[kt][c]], writes=[b_PS[pB_]], sig=(kt == 7))
                                    kb.op("act", lambda e: e.activation(out=sgA[st2][:], in_=PS[pMa][:], func=AF.Sigmoid),
                                          reads=[b_PS[pMa]], writes=[b_sgA[st2]])
                                    kb.op("act", lambda e: e.activation(out=sgB[st2][:], in_=PS[pMb][:], func=AF.Sigmoid),
                                          reads=[b_PS[pMb]], writes=[b_sgB[st2]])
                                    kb.op("dve", lambda e: e.tensor_tensor(out=tA[st2][:], in0=PS[pA][:], in1=sgA[st2][:], op=ALU.mult),
                                          reads=[b_PS[pA], b_sgA[st2]], writes=[b_tA[st2]])
                                    kb.op("dve", lambda e: e.tensor_tensor(out=tB[st2][:], in0=PS[pB_][:], in1=sgB[st2][:], op=ALU.mult),
                                          reads=[b_PS[pB_], b_sgB[st2]], writes=[b_tB[st2]])
                                    kb.op("pool", lambda e: e.tensor_tensor(out=mg[:, m, cs], in0=tA[st2][:], in1=tB[st2][:], op=ALU.add),
                                          reads=[b_tA[st2], b_tB[st2]], writes=[b_mg[m][c]])
                            kb.barrier()
                        with ExitStack() as ph:
                            PS, b_PS, _, _ = psum_banks(ph, 4)
                            fgain_bc = sb("fgain_bc", [128, D], F32, ph)
                            gate_bc = sb("gate_bc", [128, D], F32, ph)
                            xr = [sb("xr%d" % i, [128, D], F32, ph) for i in range(4)]
                            yy = [sb("yy%d" % i, [128, D], F32, ph) for i in range(5)]
                            junk2 = sb("junk2", [128, D], BF16, ph)
                            ss2 = sb("ss2", [128, NTT], F32, ph)
                            rs2 = sb("rs2", [128, NTT], F32, ph)
                            b_gain, b_gate, b_junk2 = Buf(), Buf(), Buf()
                            b_xr = [Buf() for _ in range(4)]
                            b_yy = [Buf() for _ in range(5)]
                            b_ss2 = [Buf() for _ in range(NTT)]
                            b_rs2 = [Buf() for _ in range(NTT)]
                            kb.dma("sp", fgain_bc[:], fgain_d[0:1, :].partition_broadcast(128), b_gain, writes=[b_gain])
                            kb.dma("sp", gate_bc[:], mod_d[bl:bl + 1, 2048:3072].partition_broadcast(128), b_gate, reads=[b_mod], writes=[b_gate])
                            NY = 5

                            def f2_load(tt):
                                s = tt % 4
                                r0 = tok0 + tt * 128
                                kb.dma("sp", xr[s][:], x_d[r0:r0 + 128, :], b_xr[s], writes=[b_xr[s]])

                            def f2_a(tt):
                                c = tt // 4
                                s = tt % 4
                                y3 = tt % NY
                                if tt == 0:
                                    for t0_ in range(3):
                                        f2_load(t0_)
                                if tt + 3 < NTT:
                                    f2_load(tt + 3)
                                for half in range(2):
                                    pi = 2 * (tt % 2) + half
                                    for m in range(8):
                                        kb.op("pe", lambda e: e.matmul(PS[pi][:], mg[:, m, tt * 128:(tt + 1) * 128], wout[:, m, half * 512:(half + 1) * 512],
                                                                       start=(m == 0), stop=(m == 7)),
                                              reads=[b_mg[m][c], b_wout], writes=[b_PS[pi]], sig=(m == 7))
                                    hs = slice(half * 512, (half + 1) * 512)
                                    kb.op("dve", lambda e: e.tensor_tensor(out=yy[y3][:, hs], in0=PS[pi][:], in1=gate_bc[:, hs], op=ALU.mult),
                                          reads=[b_PS[pi], b_gate], writes=[b_yy[y3]])
                                kb.op("pool", lambda e: e.tensor_tensor(out=yy[y3][:], in0=yy[y3][:], in1=xr[s][:], op=ALU.add),
                                      reads=[b_yy[y3], b_xr[s]], writes=[b_yy[y3]])
                                kb.op("act", lambda e: e.activation(out=junk2[:], in_=yy[y3][:], func=AF.Square, accum_out=ss2[:, tt:tt + 1]),
                                      reads=[b_yy[y3]], writes=[b_junk2, b_ss2[tt]])

                            def f2_b(tt):
                                kb.op("act", lambda e: e.activation(out=rs2[:, tt:tt + 1], in_=ss2[:, tt:tt + 1], func=AF.Ln, bias=cF[:, CF_EPS:CF_EPS + 1], scale=1.0 / D),
                                      reads=[b_ss2[tt]], writes=[b_rs2[tt]])
                                kb.op("act", lambda e: e.activation(out=rs2[:, tt:tt + 1], in_=rs2[:, tt:tt + 1], func=AF.Exp, scale=-0.5),
                                      reads=[b_rs2[tt]], writes=[b_rs2[tt]])
                                y3 = tt % NY
                                kb.op("act", lambda e: e.activation(out=yy[y3][:], in_=yy[y3][:], func=AF.Identity, scale=rs2[:, tt:tt + 1]),
                                      reads=[b_yy[y3], b_rs2[tt]], writes=[b_yy[y3]])

                            def f2_c(tt):
                                y3 = tt % NY
                                r0 = tok0 + tt * 128
                                kb.op("dve", lambda e: e.tensor_tensor(out=yy[y3][:], in0=yy[y3][:], in1=fgain_bc[:], op=ALU.mult),
                                      reads=[b_yy[y3], b_gain], writes=[b_yy[y3]])
                                kb.dma("sp", out_d[r0:r0 + 128, :], yy[y3][:], b_yy[y3], reads=[b_yy[y3]])

                            for i in range(NTT + 2):
                                if i < NTT:
                                    f2_a(i)
                                if 0 <= i - 1 < NTT:
                                    f2_b(i - 1)
                                if 0 <= i - 2 < NTT:
                                    f2_c(i - 2)
                            kb.barrier()
        kb.barrier()
    return nc


def _consts():
    cb = np.zeros((128, CB_N), np.float32)
    p = np.arange(128)
    cb[:, CB_ID:CB_ID + 128] = np.eye(128, dtype=np.float32)
    cb[:, CB_MTRI:CB_MTRI + 128] = (p[:, None] < p[None, :]).astype(np.float32)
    cb[:, CB_TRI:CB_TRI + 128] = -8.0 * (p[:, None] >= p[None, :]).astype(np.float32)
    for a in range(16):
        cb[:, CB_ES + a * 192 + 64 + a] = 1.0
        cb[:, CB_ES + a * 192 + 64 + 32 + a] = 1.0
    r = p % 64
    rb = np.where(r < 16, r, np.where((r >= 32) & (r < 48), r - 32, -1))
    blk = np.arange(2048) // 128
    cb[:, CB_L:CB_L + 2048] = -8.0 * ((rb[:, None] >= 0) & (blk[None, :] < rb[:, None])).astype(np.float32)
    cf = np.zeros((128, CF_N), np.float32)
    hi = (r < 32)
    cf[:, CF_SGN] = np.where(hi, 1.0, -1.0)
    cf[:, CF_MLO] = np.where(hi, 0.0, 1.0)
    cf[:, CF_NH] = -0.5
    cf[:, CF_EPS] = EPS
    for g, win in enumerate(POOL_WINDOWS):
        cf[:, CF_CNT + g * 16:CF_CNT + (g + 1) * 16] = 1.0 / np.minimum(np.arange(16) + 1, win).astype(np.float32)[None, :]
    return cb, cf


def make_in_maps(inputs, ncores=NCORES, nseq=NSEQ_CORE):
    f = lambda a: np.ascontiguousarray(np.asarray(a, dtype=np.float32))
    x = f(inputs["x"])
    c = f(inputs["c"])
    w_in = f(inputs["w_in"])[0].reshape(8, 128, 8192)
    Wp = f(w_in[:, :, :4096].reshape(8, 128, 4, 8, 128).transpose(3, 1, 0, 2, 4))
    Wpl = f(w_in[:, :, 4096:6144].reshape(8, 128, 2, 4, 2, 128).transpose(3, 1, 0, 2, 4, 5).reshape(4, 128, 8, 4, 128))
    Wm = f(w_in[:, :, 6144:8192].reshape(8, 128, 2, 8, 128).transpose(3, 1, 0, 2, 4))
    wa = f(inputs["w_branch_a"])[0].reshape(8, 128, 8, 128)
    wb = f(inputs["w_branch_b"])[0].reshape(8, 128, 8, 128)
    Wab = f(np.stack([wa, wb], axis=3).transpose(2, 1, 0, 3, 4))
    w_out_l = f(f(inputs["w_out"])[0].reshape(8, 128, 1024).transpose(1, 0, 2))
    w_ada_l = f(f(inputs["w_ada"])[0].reshape(8, 128, 3072).transpose(1, 0, 2))
    pool_w_l = f(f(inputs["pool_w"])[0].reshape(4, 2, 128, 256).transpose(2, 0, 1, 3))
    pool_scale_l = f(f(inputs["pool_scale"])[0].reshape(8, 128).T)
    cb, cf = _consts()
    shared = {
        "w_ada_l": w_ada_l, "b_ada": f(inputs["b_ada"]).reshape(1, 3072),
        "norm_gain": f(inputs["norm_gain"]).reshape(1, D), "final_gain": f(inputs["final_gain"]).reshape(1, D),
        "Wp": Wp, "Wpl": Wpl, "Wm": Wm, "Wab": Wab, "w_out_l": w_out_l, "pool_w_l": pool_w_l,
        "pool_scale_l": pool_scale_l, "constsB": cb, "constsF": cf,
    }
    maps = []
    for i in range(ncores):
        xs = x[i * nseq:(i + 1) * nseq].reshape(nseq * SEQ, D)
        cs = c[i * nseq:(i + 1) * nseq]
        cT = f(cs.reshape(nseq, 8, 128).transpose(2, 1, 0))
        m = dict(shared)
        m["x"] = f(xs)
        m["cT"] = cT
        maps.append(m)
    return maps


def kernel(**inputs):
    nc = build(NSEQ_CORE)
    in_maps = make_in_maps(inputs)
    res = run_bass_kernel_spmd(nc, in_maps, core_ids=list(range(NCORES)))
    outs = [np.asarray(r["out"], dtype=np.float32).reshape(NSEQ_CORE, SEQ, D) for r in res.results]
    return np.concatenate(outs, axis=0)
```

```python
import numpy as np
from contextlib import ExitStack
import concourse.bass as bass
import concourse.mybir as mybir
from concourse.bass_utils import run_bass_kernel_spmd

F32 = mybir.dt.float32
BF16 = mybir.dt.bfloat16
AF = mybir.ActivationFunctionType
ALU = mybir.AluOpType

NCORES = 8
SEQ = 2048
D = 1024
NSEQ_CORE = 4
EPS = 1e-6
NCH = 4
NTT = 16
POOL_WINDOWS = (2, 4, 8, 16)
USE_SOFTPLUS = True

CB_ID, CB_MTRI, CB_TRI, CB_ES, CB_L = 0, 128, 256, 384, 384 + 16 * 192
CB_N = CB_L + 2048
CF_SGN, CF_MLO, CF_CNT = 0, 1, 2
CF_NH = 2 + 64
CF_EPS = 2 + 64 + 1
CF_N = 2 + 64 + 2


class Buf:
    __slots__ = ("w", "r", "dsem", "dcnt", "uid")
    _n = 0

    def __init__(self):
        self.w = None
        self.r = {}
        self.dsem = None
        self.dcnt = 0
        Buf._n += 1
        self.uid = Buf._n


class KB:
    def __init__(self, nc, es):
        self.nc = nc
        self.es = es
        self.eng = {"pe": nc.tensor, "act": nc.scalar, "dve": nc.vector, "pool": nc.gpsimd, "sp": nc.sync}
        self.sem = {k: es.enter_context(nc.semaphore("s_" + k)) for k in self.eng}
        self.cnt = {k: 0 for k in self.eng}
        self.waited = {}
        self.dbufs = []
        self.free_sems = {True: [], False: []}
        self.nsem = 0
        self.nwaits = 0

    def _wait(self, e, tok):
        if tok is None:
            return
        key, val = tok
        if isinstance(key, str):
            if key == e and e == "pe":
                return
            k = (e, key)
            sem = self.sem[key]
        else:
            if key.dsem is None or key.dcnt < val:
                return
            k = (e, key.uid)
            sem = key.dsem
        if self.waited.get(k, 0) >= val:
            return
        self.eng[e].wait_ge(sem, val)
        self.nwaits += 1
        self.waited[k] = val

    def _deps(self, e, reads, writes):
        for b in reads:
            self._wait(e, b.w)
        for b in writes:
            self._wait(e, b.w)
            for t in b.r.values():
                self._wait(e, t)

    def op(self, e, fn, reads=(), writes=(), sig=True):
        self._deps(e, reads, writes)
        inst = fn(self.eng[e])
        if sig:
            self.cnt[e] += 1
            inst.then_inc(self.sem[e], 1)
            tok = (e, self.cnt[e])
        else:
            tok = (e, self.cnt[e] + 1)
        for b in reads:
            b.r[e] = tok
        for b in writes:
            b.w = tok
            b.r = {}
        return inst

    def dma(self, q, out, in_, sb, reads=(), writes=()):
        self._deps(q, reads, writes)
        if sb.dsem is None:
            pool_ = self.free_sems[q == "pool"]
            if pool_:
                sb.dsem, sb.dcnt = pool_.pop()
            else:
                self.nsem += 1
                sb.dsem = self.es.enter_context(self.nc.semaphore("d%d" % self.nsem))
                sb.dcnt = 0
            self.dbufs.append((sb, q == "pool"))
        inst = self.eng[q].dma_start(out=out, in_=in_)
        sb.dcnt += 16
        inst.then_inc(sb.dsem, 16)
        tok = (sb, sb.dcnt)
        for b in reads:
            b.r[("dma", sb.uid)] = tok
        for b in writes:
            b.w = tok
            b.r = {}

    def barrier(self):
        for e in self.eng:
            for f in ("pe", "act", "dve", "pool"):
                if f != e and self.cnt[f] > 0:
                    self._wait(e, (f, self.cnt[f]))
            for b, _sw in self.dbufs:
                self._wait(e, (b, b.dcnt))
        for b, sw in self.dbufs:
            self.free_sems[sw].append((b.dsem, b.dcnt))
            b.dsem = None
            b.dcnt = -1
        self.dbufs = []


def build(nseq=NSEQ_CORE):
    nc = bass.Bass("TRN2", target_bir_lowering=False)
    NTOK = nseq * SEQ

    def din(name, shape):
        return nc.dram_tensor(name, list(shape), F32, kind="ExternalInput").ap()

    x_d = din("x", [NTOK, D])
    cT_d = din("cT", [128, 8, nseq])
    wada_d = din("w_ada_l", [128, 8, 3072])
    bada_d = din("b_ada", [1, 3072])
    ngain_d = din("norm_gain", [1, D])
    fgain_d = din("final_gain", [1, D])
    wp_d = din("Wp", [8, 128, 8, 4, 128])
    wpl_d = din("Wpl", [4, 128, 8, 4, 128])
    wm_d = din("Wm", [8, 128, 8, 2, 128])
    wab_d = din("Wab", [8, 128, 8, 2, 128])
    wout_d = din("w_out_l", [128, 8, 1024])
    poolw_d = din("pool_w_l", [128, 4, 2, 256])
    pscale_d = din("pool_scale_l", [128, 8])
    cb_d = din("constsB", [128, CB_N])
    cf_d = din("constsF", [128, CF_N])
    out_d = nc.dram_tensor("out", [NTOK, D], F32, kind="ExternalOutput").ap()
    mod_d = nc.dram_tensor("mod_scratch", [nseq, 3072], F32, kind="Internal").ap()

    with ExitStack() as es:
        kb = KB(nc, es)
        uid = [0]

        def sb(name, shape, dt, stack):
            uid[0] += 1
            return stack.enter_context(nc.sbuf_tensor("%s_%d" % (name, uid[0]), list(shape), dt))

        def psum_banks(stack, n, nbf=0):
            uid[0] += 1
            ps = [stack.enter_context(nc.psum_tensor("ps%d_%d" % (i, uid[0]), [128, 512], F32)) for i in range(n)]
            pb = [stack.enter_context(nc.psum_tensor("pb%d_%d" % (i, uid[0]), [128, 8, 128], BF16)) for i in range(nbf)]
            return ps, [Buf() for _ in range(n)], pb, [Buf() for _ in range(nbf)]

        cB = sb("cB", [128, CB_N], BF16, es)
        cF = sb("cF", [128, CF_N], F32, es)
        pscale = sb("pscale", [128, 8], F32, es)
        hT = sb("hT", [128, 8, SEQ], BF16, es)

        b_cB, b_cF = Buf(), Buf()
        b_const = Buf()
        b_hT = [Buf() for _ in range(NCH)]
        b_mod = Buf()

        ident = cB[:, CB_ID:CB_ID + 128]
        mtri = cB[:, CB_MTRI:CB_MTRI + 128]
        trineg = cB[:, CB_TRI:CB_TRI + 128]

        def es_sel(a, odd):
            o = CB_ES + a * 192 + (64 if odd else 0)
            return cB[:, o:o + 128]

        kb.dma("pool", cB[:], cb_d[:, :], b_cB, writes=[b_cB])
        kb.dma("sp", cF[:], cf_d[:, :], b_cF, writes=[b_cF])
        kb.dma("sp", pscale[:], pscale_d[:, :], b_cF, writes=[b_cF])

        with ExitStack() as ph:
            PS, b_PS, _, _ = psum_banks(ph, 2)
            cTt = sb("cTt", [128, 8, nseq], F32, ph)
            scT = sb("scT", [128, 8, nseq], BF16, ph)
            wad = [sb("wad%d" % i, [128, 8, 1024], BF16, ph) for i in range(3)]
            bad = sb("bad", [nseq, 3072], F32, ph)
            modS = sb("modS", [nseq, 3072], F32, ph)
            b_c, b_sc, b_bad, b_modS = Buf(), Buf(), Buf(), Buf()
            b_wad = [Buf() for _ in range(3)]
            kb.dma("sp", cTt[:], cT_d[:, :, :], b_c, writes=[b_c])
            kb.dma("sp", bad[:], bada_d[0:1, :].partition_broadcast(nseq), b_bad, writes=[b_bad])
            for kind in range(3):
                kb.dma("pool", wad[kind][:], wada_d[:, :, kind * 1024:(kind + 1) * 1024], b_wad[kind], writes=[b_wad[kind]])
            kb.op("act", lambda e: e.activation(out=scT[:], in_=cTt[:], func=AF.Silu), reads=[b_c], writes=[b_sc])
            for kind in range(3):
                for half in range(2):
                    pb = (kind * 2 + half) % 2
                    pst = PS[pb]
                    for kt in range(8):
                        kb.op("pe", lambda e: e.matmul(pst[0:nseq, :], scT[:, kt, :], wad[kind][:, kt, half * 512:(half + 1) * 512],
                                                       start=(kt == 0), stop=(kt == 7)),
                              reads=[b_sc, b_wad[kind]], writes=[b_PS[pb]], sig=(kt == 7))
                    col = kind * 1024 + half * 512
                    kb.op("dve", lambda e: e.tensor_tensor(out=modS[:, col:col + 512], in0=pst[0:nseq, :], in1=bad[:, col:col + 512], op=ALU.add),
                          reads=[b_PS[pb], b_bad], writes=[b_modS])
            kb.dma("sp", mod_d[:, :], modS[:], b_modS, reads=[b_modS], writes=[b_mod])
            kb.barrier()

        for bl in range(nseq):
            tok0 = bl * SEQ
            with ExitStack() as ph:
                _, _, PB, b_PB = psum_banks(ph, 0, 2)
                G = sb("G", [128, D], F32, ph)
                gain_bc = sb("gain_bc", [128, D], F32, ph)
                shift = sb("shift", [128, D], F32, ph)
                sctmp = sb("sctmp", [128, D], F32, ph)
                xt = [sb("xt%d" % i, [128, D], F32, ph) for i in range(4)]
                junk = sb("junk", [128, D], BF16, ph)
                t1 = [sb("t1%d" % i, [128, D], F32, ph) for i in range(2)]
                hb = [sb("hb%d" % i, [128, D], BF16, ph) for i in range(2)]
                ss = sb("ss", [128, NTT], F32, ph)
                rstd = sb("rstd", [128, NTT], F32, ph)
                b_G, b_shift, b_sct, b_gain, b_junk = Buf(), Buf(), Buf(), Buf(), Buf()
                b_xt = [Buf() for _ in range(4)]
                b_t1 = [Buf(), Buf()]
                b_hb = [Buf(), Buf()]
                b_ss = [Buf() for _ in range(NTT)]
                b_rstd = [Buf() for _ in range(NTT)]
                kb.dma("sp", gain_bc[:], ngain_d[0:1, :].partition_broadcast(128), b_gain, writes=[b_gain])
                kb.dma("sp", shift[:], mod_d[bl:bl + 1, 0:1024].partition_broadcast(128), b_shift, reads=[b_mod], writes=[b_shift])
                kb.dma("sp", sctmp[:], mod_d[bl:bl + 1, 1024:2048].partition_broadcast(128), b_sct, reads=[b_mod], writes=[b_sct])
                kb.op("dve", lambda e: e.scalar_tensor_tensor(out=G[:], in0=sctmp[:], scalar=1.0, in1=gain_bc[:], op0=ALU.add, op1=ALU.mult),
                      reads=[b_sct, b_gain], writes=[b_G])
                def x_stage1(tt):
                    s = tt % 4
                    kb.dma("sp", xt[s][:], x_d[tok0 + tt * 128: tok0 + (tt + 1) * 128, :], b_xt[s], writes=[b_xt[s]])
                    kb.op("act", lambda e: e.activation(out=junk[:], in_=xt[s][:], func=AF.Square, accum_out=ss[:, tt:tt + 1]),
                          reads=[b_xt[s]], writes=[b_junk, b_ss[tt]])
                    kb.op("act", lambda e: e.activation(out=rstd[:, tt:tt + 1], in_=ss[:, tt:tt + 1], func=AF.Ln, bias=cF[:, CF_EPS:CF_EPS + 1], scale=1.0 / D),
                          reads=[b_ss[tt]], writes=[b_rstd[tt]])
                    kb.op("act", lambda e: e.activation(out=rstd[:, tt:tt + 1], in_=rstd[:, tt:tt + 1], func=AF.Exp, scale=-0.5),
                          reads=[b_rstd[tt]], writes=[b_rstd[tt]])

                def x_stage2(tt):
                    s = tt % 2
                    s4 = tt % 4
                    kb.op("dve", lambda e: e.scalar_tensor_tensor(out=t1[s][:], in0=xt[s4][:], scalar=rstd[:, tt:tt + 1], in1=G[:],
                                                                  op0=ALU.mult, op1=ALU.mult),
                          reads=[b_xt[s4], b_rstd[tt], b_G], writes=[b_t1[s]])
                    kb.op("pool", lambda e: e.tensor_tensor(out=hb[s][:], in0=t1[s][:], in1=shift[:], op=ALU.add),
                          reads=[b_t1[s], b_shift], writes=[b_hb[s]])

                def x_stage3(tt):
                    s = tt % 2
                    for kt in range(8):
                        kb.op("pe", lambda e: e.transpose(PB[s][:, kt, :], hb[s][:, kt * 128:(kt + 1) * 128], ident),
                              reads=[b_hb[s]], writes=[b_PB[s]], sig=(kt == 7))
                    kb.op("act", lambda e: e.activation(out=hT[:, :, tt * 128:(tt + 1) * 128], in_=PB[s][:, :, :], func=AF.Copy),
                          reads=[b_PB[s]], writes=[b_hT[tt // 4]])

                for i in range(NTT + 2):
                    if i < NTT:
                        x_stage1(i)
                    if 0 <= i - 1 < NTT:
                        x_stage2(i - 1)
                    if 0 <= i - 2 < NTT:
                        x_stage3(i - 2)
                kb.barrier()

            with ExitStack() as phga:
                ga = sb("ga", [128, 8, SEQ], BF16, phga)
                b_ga = [[Buf() for _ in range(NCH)] for _ in range(8)]
                with ExitStack() as ph:
                    PS, b_PS, _, _ = psum_banks(ph, 8)
                    PB = PS[7][:].bitcast(BF16).rearrange("p (k c) -> p k c", k=8)
                    b_PB = b_PS[7]
                    NSL = 2
                    wp = [sb("wp%d" % i, [128, 8, 4, 128], BF16, ph) for i in range(NSL)]
                    qa = [[sb("qa%d_%d" % (i, h), [128, SEQ], BF16, ph) for h in range(2)] for i in range(NSL)]
                    ka = [[sb("ka%d_%d" % (i, h), [128, SEQ], BF16, ph) for h in range(2)] for i in range(NSL)]
                    zg = [sb("zg%d" % i, [128, SEQ], F32, ph) for i in range(NSL)]
                    vT = sb("vT", [128, SEQ], BF16, ph)
                    vt = [sb("vt%d" % i, [128, NTT, 128], BF16, ph) for i in range(NSL)]
                    sp = [sb("sp%d" % h, [128, 16, 512], BF16, ph) for h in range(2)]
                    NW = 4
                    wt = [sb("wt%d" % i, [128, 512], BF16, ph) for i in range(NW)]
                    HS = sb("HS", [128, 512], BF16, ph)
                    etmp = [] if USE_SOFTPLUS else [sb("etmp%d" % i, [128, 512], F32, ph) for i in range(3)]
                    b_etmp = [Buf() for _ in range(3)]
                    b_wp = [Buf() for _ in range(NSL)]
                    b_qa = [[[Buf() for _ in range(NCH)] for _ in range(2)] for _ in range(NSL)]
                    b_ka = [[[Buf() for _ in range(NCH)] for _ in range(2)] for _ in range(NSL)]
                    b_zg = [[Buf() for _ in range(NCH)] for _ in range(NSL)]
                    b_vT = [Buf() for _ in range(NCH)]
                    b_vt = [[Buf(), Buf()] for _ in range(NSL)]
                    b_sp = [[Buf() for _ in range(16)] for _ in range(2)]
                    b_wt = [Buf() for _ in range(NW)]
                    b_HS = Buf()
                    kb.dma("pool", wp[0][:], wp_d[0], b_wp[0], writes=[b_wp[0]])
                    kb.dma("pool", wp[1][:], wp_d[1], b_wp[1], writes=[b_wp[1]])
                    for i in range(NSL):
                        kb.dma("pool", ka[i][0][64:128, :], cb_d[64:128, CB_L:CB_L + 2048], Buf(), writes=[b_ka[i][0][c] for c in range(NCH)])
                        kb.dma("pool", ka[i][1][0:64, :], cb_d[0:64, CB_L:CB_L + 2048], Buf(), writes=[b_ka[i][1][c] for c in range(NCH)])
                    state = {"ww": 0, "pp": 0}
                    XB = [1, 2, 3]
                    PROJB = [0, 7]

                    def proj_pair(j, overlapped):
                        sl = j % NSL
                        kb.op("pool", lambda e: e.memset(qa[sl][0][64:128, :], 0.0), writes=[b_qa[sl][0][c] for c in range(NCH)])
                        kb.op("pool", lambda e: e.memset(qa[sl][1][0:64, :], 0.0), writes=[b_qa[sl][1][c] for c in range(NCH)])
                        for g in range(4):
                            for c in range(NCH):
                                pi = PROJB[state["pp"] % 2]
                                state["pp"] += 1
                                pst = PS[pi]
                                for kt in range(8):
                                    kb.op("pe", lambda e: e.matmul(pst[:], wp[sl][:, kt, g, :], hT[:, kt, c * 512:(c + 1) * 512],
                                                                   start=(kt == 0), stop=(kt == 7)),
                                          reads=[b_wp[sl], b_hT[c]], writes=[b_PS[pi]], sig=(kt == 7))
                                    if overlapped and kt == 3:
                                        yield
                                cs = slice(c * 512, (c + 1) * 512)
                                if g == 0 or g == 1:
                                    T, bT = (qa, b_qa) if g == 0 else (ka, b_ka)
                                    kb.op("dve", lambda e: e.tensor_copy(out=T[sl][0][0:64, cs], in_=pst[0:64, :]),
                                          reads=[b_PS[pi]], writes=[bT[sl][0][c]])
                                    kb.op("dve", lambda e: e.tensor_copy(out=T[sl][1][64:128, cs], in_=pst[64:128, :]),
                                          reads=[b_PS[pi]], writes=[bT[sl][1][c]])
                                elif g == 2:
                                    kb.op("dve", lambda e: e.tensor_copy(out=vT[:, cs], in_=pst[:]),
                                          reads=[b_PS[pi]], writes=[b_vT[c]])
                                else:
                                    kb.op("dve", lambda e: e.tensor_copy(out=zg[sl][:, cs], in_=pst[:]),
                                          reads=[b_PS[pi]], writes=[b_zg[sl][c]])
                                yield
                        for hf in range(2):
                            for k8 in range(8):
                                kbi = hf * 8 + k8
                                kb.op("pe", lambda e: e.transpose(PB[:, k8, :], vT[:, kbi * 128:(kbi + 1) * 128], ident),
                                      reads=[b_vT[kbi // 4]], writes=[b_PB], sig=(k8 == 7))
                            kb.op("dve", lambda e: e.tensor_copy(out=vt[sl][:, hf * 8:(hf + 1) * 8, :], in_=PB[:, :, :]),
                                  reads=[b_PB], writes=[b_vt[sl][hf]])
                            yield

                    def silu_zg(sl):
                        kb.op("act", lambda e: e.activation(out=zg[sl][:, :], in_=zg[sl][:, :], func=AF.Silu),
                              reads=[b_zg[sl][c] for c in range(NCH)], writes=[b_zg[sl][c] for c in range(NCH)])

                    def attn_pair(j):
                        sl = j % NSL
                        SK = 2
                        for c in range(NCH):
                            n = 4 * c + 4
                            cbase = c * 512
                            blocks = [(hd, a) for a in range(n) for hd in range(2)]
                            NB = len(blocks)

                            def lo_of(a):
                                return max(0, a - 4 * c) * 128

                            def p1_front(i):
                                hd, a = blocks[i]
                                lo = lo_of(a)
                                xi = XB[i % len(XB)]
                                X = PS[xi]
                                kb.op("pe", lambda e: e.matmul(X[:, lo:512], ka[sl][hd][:, a * 128:(a + 1) * 128],
                                                               qa[sl][hd][:, cbase + lo:cbase + 512], start=True, stop=True),
                                      reads=[b_ka[sl][hd][a // 4], b_qa[sl][hd][c]], writes=[b_PS[xi]])
                                if USE_SOFTPLUS:
                                    kb.op("act", lambda e: e.activation(out=sp[hd][:, a, lo:512], in_=X[:, lo:512], func=AF.Softplus, scale=0.125),
                                          reads=[b_PS[xi]], writes=[b_sp[hd][a]])
                                else:
                                    ei = i % 3
                                    kb.op("act", lambda e: e.activation(out=etmp[ei][:, lo:512], in_=X[:, lo:512], func=AF.Exp, scale=0.125),
                                          reads=[b_PS[xi]], writes=[b_etmp[ei]])
                                    kb.op("act", lambda e: e.activation(out=sp[hd][:, a, lo:512], in_=etmp[ei][:, lo:512], func=AF.Ln, bias=1.0, scale=1.0),
                                          reads=[b_etmp[ei]], writes=[b_sp[hd][a]])
                                if a >= 4 * c:
                                    kb.op("dve", lambda e: e.tensor_tensor(out=sp[hd][:, a, lo:lo + 128], in0=sp[hd][:, a, lo:lo + 128],
                                                                           in1=mtri, op=ALU.mult),
                                          reads=[b_sp[hd][a]], writes=[b_sp[hd][a]])

                            def p1_back(i):
                                hd, a = blocks[i]
                                lo = lo_of(a)
                                kb.op("pe", lambda e: e.matmul(PS[4][:, lo:512], es_sel(a, hd == 1), sp[hd][:, a, lo:512],
                                                               start=(i == 0), stop=(i == NB - 1)),
                                      reads=[b_sp[hd][a]], writes=[b_PS[4]], sig=(i == NB - 1))

                            for i in range(NB + SK):
                                if i < NB:
                                    p1_front(i)
                                if i - SK >= 0:
                                    p1_back(i - SK)
                                yield "step1"
                            kb.op("dve", lambda e: e.tensor_scalar(out=HS[:], in0=PS[4][:], scalar1=cF[:, CF_SGN:CF_SGN + 1], scalar2=None,
                                                                   op0=ALU.mult), reads=[b_PS[4]], writes=[b_HS])
                            kb.op("dve", lambda e: e.scalar_tensor_tensor(out=qa[sl][0][64:128, cbase:cbase + 512], in0=PS[4][64:128, :],
                                                                          scalar=cF[64:128, CF_MLO:CF_MLO + 1], in1=HS[64:128, :],
                                                                          op0=ALU.mult, op1=ALU.add),
                                  reads=[b_PS[4], b_HS], writes=[b_qa[sl][0][c]])
                            kb.op("dve", lambda e: e.scalar_tensor_tensor(out=qa[sl][1][0:64, cbase:cbase + 512], in0=PS[4][0:64, :],
                                                                          scalar=cF[0:64, CF_MLO:CF_MLO + 1], in1=HS[0:64, :],
                                                                          op0=ALU.mult, op1=ALU.add),
                                  reads=[b_PS[4], b_HS], writes=[b_qa[sl][1][c]])
                            if c == 0:
                                silu_zg(sl)
                            yield "drain"
                            wslot = {}

                            def p3_front(i):
                                hd, a = blocks[i]
                                lo = lo_of(a)
                                xi = XB[i % len(XB)]
                                X = PS[xi]
                                kb.op("pe", lambda e: e.matmul(X[:, lo:512], ka[sl][hd][:, a * 128:(a + 1) * 128],
                                                               qa[sl][hd][:, cbase + lo:cbase + 512], start=True, stop=False),
                                      reads=[b_ka[sl][hd][a // 4], b_qa[sl][hd][c]], writes=[b_PS[xi]], sig=False)
                                kb.op("pe", lambda e: e.matmul(X[:, lo:512], trineg, sp[hd][:, a, lo:512], start=False, stop=True),
                                      reads=[b_sp[hd][a]], writes=[b_PS[xi]])
                                wi = state["ww"] % NW
                                state["ww"] += 1
                                wslot[i] = wi
                                kb.op("act", lambda e: e.activation(out=wt[wi][:, lo:512], in_=X[:, lo:512], func=AF.Exp, scale=0.125),
                                      reads=[b_PS[xi]], writes=[b_wt[wi]])
                                if a >= 4 * c:
                                    kb.op("dve", lambda e: e.tensor_tensor(out=wt[wi][:, lo:lo + 128], in0=wt[wi][:, lo:lo + 128],
                                                                           in1=mtri, op=ALU.mult),
                                          reads=[b_wt[wi]], writes=[b_wt[wi]])

                            def p3_back(i):
                                hd, a = blocks[i]
                                lo = lo_of(a)
                                wi = wslot[i]
                                oi = 5 + hd
                                rows = slice(0, 64) if hd == 0 else slice(64, 128)
                                kb.op("pe", lambda e: e.matmul(PS[oi][:, lo:512], vt[sl][:, a, :], wt[wi][:, lo:512],
                                                               start=(a == 0), stop=(a == n - 1)),
                                      reads=[b_vt[sl][a // 8], b_wt[wi]], writes=[b_PS[oi]], sig=(a == n - 1))
                                if a == n - 1:
                                    kb.op("dve", lambda e: e.tensor_tensor(out=ga[rows, j, cbase:cbase + 512], in0=PS[oi][rows, :],
                                                                           in1=zg[sl][rows, cbase:cbase + 512], op=ALU.mult),
                                          reads=[b_PS[oi], b_zg[sl][c]], writes=[b_ga[j][c]])

                            for i in range(NB + SK):
                                if i < NB:
                                    p3_front(i)
                                if i - SK >= 0:
                                    p3_back(i - SK)
                                yield "step"
                            yield "drain"

                    def run_interleaved(gmain, gside, ratio, ndrain):
                        k = 0
                        side_done = gside is None

                        def side(nsteps):
                            nonlocal side_done
                            for _ in range(nsteps):
                                if side_done:
                                    return
                                try:
                                    next(gside)
                                except StopIteration:
                                    side_done = True

                        for tag in gmain:
                            if tag == "drain":
                                side(ndrain)
                            elif tag == "step1":
                                k += 1
                                if k % ratio == 0:
                                    side(1)
                        if not side_done:
                            for _ in gside:
                                pass

                    for _ in proj_pair(0, False):
                        pass
                    for j in range(8):
                        if j + 2 < 8:
                            kb.dma("pool", wp[j % NSL][:], wp_d[j + 2], b_wp[j % NSL], writes=[b_wp[j % NSL]])
                        nxt = proj_pair(j + 1, True) if j + 1 < 8 else None
                        run_interleaved(attn_pair(j), nxt, 8, 3)
                    kb.barrier()

                with ExitStack() as phgb:
                    gb = sb("gb", [128, 8, SEQ], BF16, phgb)
                    b_gb = [[Buf() for _ in range(NCH)] for _ in range(8)]
                    with ExitStack() as ph:
                        PS, b_PS, _, _ = psum_banks(ph, 6)
                        wg = [sb("wg%d" % i, [128, 8, 4, 128], BF16, ph) for i in range(2)]
                        poolw = sb("poolw", [128, 4, 2, 256], BF16, ph)
                        U = [[sb("U%d_%d" % (s_, ci), [128, 16 + 512], F32, ph) for ci in range(2)] for s_ in range(2)]
                        SA = [sb("SA%d" % ci, [128, 16 + 512], F32, ph) for ci in range(2)]
                        SB_ = [sb("SB%d" % ci, [128, 16 + 512], F32, ph) for ci in range(2)]
                        dT = [[sb("dT%d_%d" % (s_, ci), [128, 512], BF16, ph) for ci in range(2)] for s_ in range(2)]
                        zbs = [[sb("zbs%d_%d" % (s_, dt_), [128, 512], F32, ph) for dt_ in range(2)] for s_ in range(2)]
                        fx = sb("fx", [128, 16], F32, ph)
                        b_wg = [Buf(), Buf()]
                        b_poolw, b_fx = Buf(), Buf()
                        b_U = [[Buf(), Buf()], [Buf(), Buf()]]
                        b_SA = [Buf(), Buf()]
                        b_SB = [Buf(), Buf()]
                        b_dT = [[Buf(), Buf()], [Buf(), Buf()]]
                        b_zbs = [[Buf(), Buf()], [Buf(), Buf()]]
                        kb.dma("pool", poolw[:], poolw_d[:, :, :, :], b_poolw, writes=[b_poolw])
                        state_p = {"pp": 0}

                        def p_stage_a(g, c, s_):
                            win = POOL_WINDOWS[g]
                            gs = g % 2
                            cs = slice(c * 512, (c + 1) * 512)
                            if c == 0:
                                if g == 0:
                                    kb.dma("pool", wg[0][:], wpl_d[0], b_wg[0], writes=[b_wg[0]])
                                if g + 1 < 4:
                                    kb.dma("pool", wg[(g + 1) % 2][:], wpl_d[g + 1], b_wg[(g + 1) % 2], writes=[b_wg[(g + 1) % 2]])
                                for ci in range(2):
                                    kb.op("pool", lambda e: e.memset(U[s_][ci][:, 0:16], 0.0), writes=[b_U[s_][ci]])
                            for which in range(4):
                                pi = state_p["pp"] % 4
                                state_p["pp"] += 1
                                pst = PS[pi]
                                for kt in range(8):
                                    kb.op("pe", lambda e: e.matmul(pst[:], wg[gs][:, kt, which, :], hT[:, kt, cs],
                                                                   start=(kt == 0), stop=(kt == 7)),
                                          reads=[b_wg[gs], b_hT[c]], writes=[b_PS[pi]], sig=(kt == 7))
                                if which < 2:
                                    ci = which
                                    kb.op("act", lambda e: e.activation(out=U[s_][ci][:, 16:528], in_=pst[:], func=AF.Copy),
                                          reads=[b_PS[pi]], writes=[b_U[s_][ci]])
                                else:
                                    dt_ = which - 2
                                    kb.op("act", lambda e: e.activation(out=zbs[s_][dt_][:], in_=pst[:], func=AF.Silu),
                                          reads=[b_PS[pi]], writes=[b_zbs[s_][dt_]])
                            for ci in range(2):
                                eng = "pool" if ci == 0 else "dve"
                                u = U[s_][ci]
                                bu = b_U[s_][ci]
                                if c + 1 < NCH:
                                    kb.op("pool", lambda e: e.tensor_copy(out=U[1 - s_][ci][:, 0:16], in_=u[:, 512:528]),
                                          reads=[bu], writes=[b_U[1 - s_][ci]])
                                cur, bcur = u, bu
                                nxts = [(SA[ci], b_SA[ci]), (SB_[ci], b_SB[ci])]
                                first = 0
                                for step in range(g + 1):
                                    sh = 1 << step
                                    nx, bnx = nxts[step % 2]
                                    f0 = first + sh
                                    kb.op(eng, lambda e: e.tensor_tensor(out=nx[:, f0:528], in0=cur[:, f0:528],
                                                                         in1=cur[:, f0 - sh:528 - sh], op=ALU.add),
                                          reads=[bcur], writes=[bnx])
                                    cur, bcur = nx, bnx
                                    first = f0
                                kb.op("dve", lambda e: e.scalar_tensor_tensor(out=dT[s_][ci][:], in0=cur[:, 16:528], scalar=1.0 / win,
                                                                              in1=u[:, 16:528], op0=ALU.mult, op1=ALU.subtract),
                                      reads=[bcur, bu], writes=[b_dT[s_][ci]])
                                if c == 0:
                                    kb.op("dve", lambda e: e.tensor_tensor(out=fx[:], in0=cur[:, 16:32],
                                                                           in1=cF[:, CF_CNT + g * 16:CF_CNT + (g + 1) * 16], op=ALU.mult),
                                          reads=[bcur], writes=[b_fx])
                                    kb.op("dve", lambda e: e.tensor_tensor(out=dT[s_][ci][:, 0:16], in0=fx[:], in1=u[:, 16:32], op=ALU.subtract),
                                          reads=[b_fx, bu, b_dT[s_][ci]], writes=[b_dT[s_][ci]])

                        def p_stage_b(g, c, s_):
                            cs = slice(c * 512, (c + 1) * 512)
                            for dt_ in range(2):
                                ft = 2 * g + dt_
                                pi = 4 + dt_
                                pst = PS[pi]
                                for ci in range(2):
                                    kb.op("pe", lambda e: e.matmul(pst[:], poolw[:, g, ci, dt_ * 128:(dt_ + 1) * 128], dT[s_][ci][:],
                                                                   start=(ci == 0), stop=(ci == 1)),
                                          reads=[b_poolw, b_dT[s_][ci]], writes=[b_PS[pi]], sig=(ci == 1))
                                kb.op("dve", lambda e: e.scalar_tensor_tensor(out=gb[:, ft, cs], in0=pst[:],
                                                                              scalar=pscale[:, ft:ft + 1], in1=zbs[s_][dt_][:],
                                                                              op0=ALU.mult, op1=ALU.mult),
                                      reads=[b_PS[pi], b_zbs[s_][dt_]], writes=[b_gb[ft][c]])

                        steps = [(g, c) for g in range(4) for c in range(NCH)]
                        for k in range(len(steps) + 1):
                            if k < len(steps):
                                p_stage_a(steps[k][0], steps[k][1], k % 2)
                            if k - 1 >= 0:
                                p_stage_b(steps[k - 1][0], steps[k - 1][1], (k - 1) % 2)
                        kb.barrier()

                    with ExitStack() as phf:
                        mg = sb("mg", [128, 8, SEQ], BF16, phf)
                        wout = sb("wout", [128, 8, 1024], BF16, phf)
                        b_mg = [[Buf() for _ in range(NCH)] for _ in range(8)]
                        b_wout = Buf()
                        with ExitStack() as ph:
                            PS, b_PS, _, _ = psum_banks(ph, 8)
                            NS3 = 3
                            wab = [sb("wab%d" % i, [128, 8, 2, 128], BF16, ph) for i in range(NS3)]
                            wm = [sb("wm%d" % i, [128, 8, 2, 128], BF16, ph) for i in range(NS3)]
                            sgA = [sb("sgA%d" % i, [128, 512], F32, ph) for i in range(2)]
                            sgB = [sb("sgB%d" % i, [128, 512], F32, ph) for i in range(2)]
                            tA = [sb("tA%d" % i, [128, 512], F32, ph) for i in range(2)]
                            tB = [sb("tB%d" % i, [128, 512], F32, ph) for i in range(2)]
                            b_wab = [Buf() for _ in range(NS3)]
                            b_wm = [Buf() for _ in range(NS3)]
                            b_sgA, b_sgB, b_tA, b_tB = [Buf(), Buf()], [Buf(), Buf()], [Buf(), Buf()], [Buf(), Buf()]

                            def load_m(m):
                                s3 = m % NS3
                                kb.dma("pool", wab[s3][:], wab_d[m], b_wab[s3], writes=[b_wab[s3]])
                                kb.dma("pool", wm[s3][:], wm_d[m], b_wm[s3], writes=[b_wm[s3]])

                            load_m(0)
                            load_m(1)
                            step = 0
                            for m in range(8):
                                s3 = m % NS3
                                if m + 2 < 8:
                                    load_m(m + 2)
                                if m == 1:
                                    kb.dma("pool", wout[:], wout_d[:, :, :], b_wout, writes=[b_wout])
                                for c in range(NCH):
                                    cs = slice(c * 512, (c + 1) * 512)
                                    st2 = step % 2
                                    step += 1
                                    pA, pB_, pMa, pMb = [4 * st2 + k_ for k_ in range(4)]
                                    for kt in range(8):
                                        kb.op("pe", lambda e: e.matmul(PS[pMa][:], wm[s3][:, kt, 0, :], hT[:, kt, cs], start=(kt == 0), stop=(kt == 7)),
                                              reads=[b_wm[s3], b_hT[c]], writes=[b_PS[pMa]], sig=(kt == 7))
                                    for kt in range(8):
                                        kb.op("pe", lambda e: e.matmul(PS[pMb][:], wm[s3][:, kt, 1, :], hT[:, kt, cs], start=(kt == 0), stop=(kt == 7)),
                                              reads=[b_wm[s3], b_hT[c]], writes=[b_PS[pMb]], sig=(kt == 7))
                                    for kt in range(8):
                                        kb.op("pe", lambda e: e.matmul(PS[pA][:], wab[s3][:, kt, 0, :], ga[:, kt, cs], start=(kt == 0), stop=(kt == 7)),
                                              reads=[b_wab[s3], b_ga[kt][c]], writes=[b_PS[pA]], sig=(kt == 7))
                                    for kt in range(8):
                                        kb.op("pe", lambda e: e.matmul(PS[pB_][:], wab[s3][:, kt, 1, :], gb[:, kt, cs], start=(kt == 0), stop=(kt == 7)),
                                              reads=[b_wab[s3], b_gb[kt][c]], writes=[b_PS[pB_]], sig=(kt == 7))
                                    kb.op("act", lambda e: e.activation(out=sgA[st2][:], in_=PS[pMa][:], func=AF.Sigmoid),
                                          reads=[b_PS[pMa]], writes=[b_sgA[st2]])
                                    kb.op("act", lambda e: e.activation(out=sgB[st2][:], in_=PS[pMb][:], func=AF.Sigmoid),
                                          reads=[b_PS[pMb]], writes=[b_sgB[st2]])
                                    kb.op("dve", lambda e: e.tensor_tensor(out=tA[st2][:], in0=PS[pA][:], in1=sgA[st2][:], op=ALU.mult),
                                          reads=[b_PS[pA], b_sgA[st2]], writes=[b_tA[st2]])
                                    kb.op("dve", lambda e: e.tensor_tensor(out=tB[st2][:], in0=PS[pB_][:], in1=sgB[st2][:], op=ALU.mult),
                                          reads=[b_PS[pB_], b_sgB[st2]], writes=[b_tB[st2]])
                                    kb.op("pool", lambda e: e.tensor_tensor(out=mg[:, m, cs], in0=tA[st2][:], in1=tB[st2][:], op=ALU.add),
                                          reads=[b_tA[st2], b_tB[st2]], writes=[b_mg[m][c]])
                            kb.barrier()
                        with ExitStack() as ph:
                            PS, b_PS, _, _ = psum_banks(ph, 4)
                            fgain_bc = sb("fgain_bc", [128, D], F32, ph)
                            gate_bc = sb("gate_bc", [128, D], F32, ph)
                            xr = [sb("xr%d" % i, [128, D], F32, ph) for i in range(4)]
                            yy = [sb("yy%d" % i, [128, D], F32, ph) for i in range(5)]
                            junk2 = sb("junk2", [128, D], BF16, ph)
                            ss2 = sb("ss2", [128, NTT], F32, ph)
                            rs2 = sb("rs2", [128, NTT], F32, ph)
                            b_gain, b_gate, b_junk2 = Buf(), Buf(), Buf()
                            b_xr = [Buf() for _ in range(4)]
                            b_yy = [Buf() for _ in range(5)]
                            b_ss2 = [Buf() for _ in range(NTT)]
                            b_rs2 = [Buf() for _ in range(NTT)]
                            kb.dma("sp", fgain_bc[:], fgain_d[0:1, :].partition_broadcast(128), b_gain, writes=[b_gain])
                            kb.dma("sp", gate_bc[:], mod_d[bl:bl + 1, 2048:3072].partition_broadcast(128), b_gate, reads=[b_mod], writes=[b_gate])
                            NY = 5

                            def f2_load(tt):
                                s = tt % 4
                                r0 = tok0 + tt * 128
                                kb.dma("sp", xr[s][:], x_d[r0:r0 + 128, :], b_xr[s], writes=[b_xr[s]])

                            def f2_a(tt):
                                c = tt // 4
                                s = tt % 4
                                y3 = tt % NY
                                if tt == 0:
                                    for t0_ in range(3):
                                        f2_load(t0_)
                                if tt + 3 < NTT:
                                    f2_load(tt + 3)
                                for half in range(2):
                                    pi = 2 * (tt % 2) + half
                                    for m in range(8):
                                        kb.op("pe", lambda e: e.matmul(PS[pi][:], mg[:, m, tt * 128:(tt + 1) * 128], wout[:, m, half * 512:(half + 1) * 512],
                                                                       start=(m == 0), stop=(m == 7)),
                                              reads=[b_mg[m][c], b_wout], writes=[b_PS[pi]], sig=(m == 7))
                                    hs = slice(half * 512, (half + 1) * 512)
                                    kb.op("dve", lambda e: e.tensor_tensor(out=yy[y3][:, hs], in0=PS[pi][:], in1=gate_bc[:, hs], op=ALU.mult),
                                          reads=[b_PS[pi], b_gate], writes=[b_yy[y3]])
                                kb.op("pool", lambda e: e.tensor_tensor(out=yy[y3][:], in0=yy[y3][:], in1=xr[s][:], op=ALU.add),
                                      reads=[b_yy[y3], b_xr[s]], writes=[b_yy[y3]])
                                kb.op("act", lambda e: e.activation(out=junk2[:], in_=yy[y3][:], func=AF.Square, accum_out=ss2[:, tt:tt + 1]),
                                      reads=[b_yy[y3]], writes=[b_junk2, b_ss2[tt]])

                            def f2_b(tt):
                                kb.op("act", lambda e: e.activation(out=rs2[:, tt:tt + 1], in_=ss2[:, tt:tt + 1], func=AF.Ln, bias=cF[:, CF_EPS:CF_EPS + 1], scale=1.0 / D),
                                      reads=[b_ss2[tt]], writes=[b_rs2[tt]])
                                kb.op("act", lambda e: e.activation(out=rs2[:, tt:tt + 1], in_=rs2[:, tt:tt + 1], func=AF.Exp, scale=-0.5),
                                      reads=[b_rs2[tt]], writes=[b_rs2[tt]])
                                y3 = tt % NY
                                kb.op("act", lambda e: e.activation(out=yy[y3][:], in_=yy[y3][:], func=AF.Identity, scale=rs2[:, tt:tt + 1]),
                                      reads=[b_yy[y3], b_rs2[tt]], writes=[b_yy[y3]])

                            def f2_c(tt):
                                y3 = tt % NY
                                r0 = tok0 + tt * 128
                                kb.op("dve", lambda e: e.tensor_tensor(out=yy[y3][:], in0=yy[y3][:], in1=fgain_bc[:], op=ALU.mult),
                                      reads=[b_yy[y3], b_gain], writes=[b_yy[y3]])
                                kb.dma("sp", out_d[r0:r0 + 128, :], yy[y3][:], b_yy[y3], reads=[b_yy[y3]])

                            for i in range(NTT + 2):
                                if i < NTT:
                                    f2_a(i)
                                if 0 <= i - 1 < NTT:
                                    f2_b(i - 1)
                                if 0 <= i - 2 < NTT:
                                    f2_c(i - 2)
                            kb.barrier()
        kb.barrier()
    return nc


def _consts():
    cb = np.zeros((128, CB_N), np.float32)
    p = np.arange(128)
    cb[:, CB_ID:CB_ID + 128] = np.eye(128, dtype=np.float32)
    cb[:, CB_MTRI:CB_MTRI + 128] = (p[:, None] < p[None, :]).astype(np.float32)
    cb[:, CB_TRI:CB_TRI + 128] = -8.0 * (p[:, None] >= p[None, :]).astype(np.float32)
    for a in range(16):
        cb[:, CB_ES + a * 192 + 64 + a] = 1.0
        cb[:, CB_ES + a * 192 + 64 + 32 + a] = 1.0
    r = p % 64
    rb = np.where(r < 16, r, np.where((r >= 32) & (r < 48), r - 32, -1))
    blk = np.arange(2048) // 128
    cb[:, CB_L:CB_L + 2048] = -8.0 * ((rb[:, None] >= 0) & (blk[None, :] < rb[:, None])).astype(np.float32)
    cf = np.zeros((128, CF_N), np.float32)
    hi = (r < 32)
    cf[:, CF_SGN] = np.where(hi, 1.0, -1.0)
    cf[:, CF_MLO] = np.where(hi, 0.0, 1.0)
    cf[:, CF_NH] = -0.5
    cf[:, CF_EPS] = EPS
    for g, win in enumerate(POOL_WINDOWS):
        cf[:, CF_CNT + g * 16:CF_CNT + (g + 1) * 16] = 1.0 / np.minimum(np.arange(16) + 1, win).astype(np.float32)[None, :]
    return cb, cf


def make_in_maps(inputs, ncores=NCORES, nseq=NSEQ_CORE):
    f = lambda a: np.ascontiguousarray(np.asarray(a, dtype=np.float32))
    x = f(inputs["x"])
    c = f(inputs["c"])
    w_in = f(inputs["w_in"])[0].reshape(8, 128, 8192)
    Wp = f(w_in[:, :, :4096].reshape(8, 128, 4, 8, 128).transpose(3, 1, 0, 2, 4))
    Wpl = f(w_in[:, :, 4096:6144].reshape(8, 128, 2, 4, 2, 128).transpose(3, 1, 0, 2, 4, 5).reshape(4, 128, 8, 4, 128))
    Wm = f(w_in[:, :, 6144:8192].reshape(8, 128, 2, 8, 128).transpose(3, 1, 0, 2, 4))
    wa = f(inputs["w_branch_a"])[0].reshape(8, 128, 8, 128)
    wb = f(inputs["w_branch_b"])[0].reshape(8, 128, 8, 128)
    Wab = f(np.stack([wa, wb], axis=3).transpose(2, 1, 0, 3, 4))
    w_out_l = f(f(inputs["w_out"])[0].reshape(8, 128, 1024).transpose(1, 0, 2))
    w_ada_l = f(f(inputs["w_ada"])[0].reshape(8, 128, 3072).transpose(1, 0, 2))
    pool_w_l = f(f(inputs["pool_w"])[0].reshape(4, 2, 128, 256).transpose(2, 0, 1, 3))
    pool_scale_l = f(f(inputs["pool_scale"])[0].reshape(8, 128).T)
    cb, cf = _consts()
    shared = {
        "w_ada_l": w_ada_l, "b_ada": f(inputs["b_ada"]).reshape(1, 3072),
        "norm_gain": f(inputs["norm_gain"]).reshape(1, D), "final_gain": f(inputs["final_gain"]).reshape(1, D),
        "Wp": Wp, "Wpl": Wpl, "Wm": Wm, "Wab": Wab, "w_out_l": w_out_l, "pool_w_l": pool_w_l,
        "pool_scale_l": pool_scale_l, "constsB": cb, "constsF": cf,
    }
    maps = []
    for i in range(ncores):
        xs = x[i * nseq:(i + 1) * nseq].reshape(nseq * SEQ, D)
        cs = c[i * nseq:(i + 1) * nseq]
        cT = f(cs.reshape(nseq, 8, 128).transpose(2, 1, 0))
        m = dict(shared)
        m["x"] = f(xs)
        m["cT"] = cT
        maps.append(m)
    return maps


def kernel(**inputs):
    nc = build(NSEQ_CORE)
    in_maps = make_in_maps(inputs)
    res = run_bass_kernel_spmd(nc, in_maps, core_ids=list(range(NCORES)))
    outs = [np.asarray(r["out"], dtype=np.float32).reshape(NSEQ_CORE, SEQ, D) for r in res.results]
    return np.concatenate(outs, axis=0)
```
